# Optimizing a Trainium2 kernel written in Bass

```python
import math
import jax, jax.numpy as jnp
from jax import lax
import numpy as np

D_MODEL = 1024
BATCH = 4
SEQ = 4096
DEPTH = 2

GRID_W = 64
CTX_LEN = 256
N_EVEN = (DEPTH + 1) // 2
N_ODD = DEPTH // 2
HEAD_DIM = 64
ROPE_THETA = 10000.0
EPS = 1e-6
BLOCK = 128
A_HEADS = 8
A_KV_HEADS = 2
A_GROUP = A_HEADS // A_KV_HEADS
A_WINDOW = 128
B_HEADS = 4
B_HEAD_DIM = 64
B_LAMBDA_DECAY = 0.3
A_Q = A_HEADS * HEAD_DIM
A_KV = A_KV_HEADS * HEAD_DIM
B_QK = B_HEADS * 2 * B_HEAD_DIM
B_V = B_HEADS * 2 * B_HEAD_DIM
ATTN_SPLITS = (A_Q, A_Q + A_KV, A_Q + 2 * A_KV, A_Q + 2 * A_KV + B_QK, A_Q + 2 * A_KV + 2 * B_QK)
ATTN_IN = A_Q + 2 * A_KV + 2 * B_QK + B_V
ATTN_OUT = A_HEADS * HEAD_DIM + B_HEADS * 2 * B_HEAD_DIM
C_HEADS = 4
C_DK = D_MODEL // 2 // C_HEADS
C_DV = D_MODEL // C_HEADS
C_GATE_RANK = 16
C_GATE_NORM = 16.0
C_CHUNK = 64
GLA_SPLITS = (C_HEADS * C_DK, 2 * C_HEADS * C_DK, 2 * C_HEADS * C_DK + C_HEADS * C_DV)
GLA_IN = 2 * C_HEADS * C_DK + 2 * C_HEADS * C_DV
D_FF = 2816
CONV_W = 3

kernel_name = 'hybrid_dit_swa_diff_gla_convffn'


def rms_norm(x, g):
    xf = x.astype(jnp.float32)
    y = xf * lax.rsqrt(jnp.mean(xf * xf, axis=-1, keepdims=True) + EPS)
    return (y * g.astype(jnp.float32)).astype(x.dtype)


def axial_rope_tables(n_tok):
    rows = n_tok // GRID_W
    row = jnp.repeat(jnp.arange(rows, dtype=jnp.float32), GRID_W)
    col = jnp.tile(jnp.arange(GRID_W, dtype=jnp.float32), rows)
    axis_dim = HEAD_DIM // 2
    inv_freq = ROPE_THETA ** (-jnp.arange(0, axis_dim, 2, dtype=jnp.float32) / axis_dim)
    ang = jnp.concatenate([row[:, None] * inv_freq, col[:, None] * inv_freq], axis=-1)
    return jnp.cos(ang), jnp.sin(ang)


def apply_rope(x, cos, sin):
    bshape = (cos.shape[0],) + (1,) * (x.ndim - 3) + (cos.shape[1],)
    c = cos.reshape(bshape).astype(x.dtype)
    s = sin.reshape(bshape).astype(x.dtype)
    x1, x2 = x[..., 0::2], x[..., 1::2]
    return jnp.stack([x1 * c - x2 * s, x1 * s + x2 * c], axis=-1).reshape(x.shape)


def window_gqa_latent(q, k, v, kc, vc, sink):
    bsz, n_tok = q.shape[:2]
    nb = n_tok // BLOCK
    scale = HEAD_DIM ** -0.5
    qb = q.reshape(bsz, nb, BLOCK, A_KV_HEADS, A_GROUP, HEAD_DIM)

    def band(a):
        a = jnp.pad(a, ((0, 0), (BLOCK, BLOCK), (0, 0), (0, 0))).reshape(bsz, nb + 2, BLOCK, A_KV_HEADS, HEAD_DIM)
        return jnp.concatenate([a[:, :-2], a[:, 1:-1], a[:, 2:]], axis=2)

    kb, vb = band(k), band(v)
    s_win = jnp.einsum('bnqkgd,bnskd->bnkgqs', qb, kb).astype(jnp.float32) * scale
    blk = jnp.arange(nb)[:, None] * BLOCK
    qpos = blk + jnp.arange(BLOCK)[None, :]
    kpos = blk - BLOCK + jnp.arange(3 * BLOCK)[None, :]
    valid = ((kpos[:, None, :] >= 0) & (kpos[:, None, :] < n_tok)
             & (jnp.abs(qpos[:, :, None] - kpos[:, None, :]) <= A_WINDOW))
    s_win = jnp.where(valid[None, :, None, None], s_win, -jnp.inf)
    s_ctx = jnp.einsum('bnqkgd,bskd->bnkgqs', qb, kc).astype(jnp.float32) * scale
    s_sink = jnp.broadcast_to(sink.astype(jnp.float32).reshape(1, 1, A_KV_HEADS, A_GROUP, 1, 1), s_ctx.shape[:-1] + (1,))
    p = jax.nn.softmax(jnp.concatenate([s_win, s_ctx, s_sink], axis=-1), axis=-1).astype(v.dtype)
    n_ctx = kc.shape[1]
    o = (jnp.einsum('bnkgqs,bnskd->bnqkgd', p[..., :3 * BLOCK], vb)
         + jnp.einsum('bnkgqs,bskd->bnqkgd', p[..., 3 * BLOCK:3 * BLOCK + n_ctx], vc))
    return o.reshape(bsz, n_tok, A_HEADS * HEAD_DIM)


def gqa_context(qc, kc, vc, sink):
    bsz, n_ctx = qc.shape[:2]
    qg = qc.reshape(bsz, n_ctx, A_KV_HEADS, A_GROUP, HEAD_DIM)
    s = jnp.einsum('bqkgd,bskd->bkgqs', qg, kc).astype(jnp.float32) * (HEAD_DIM ** -0.5)
    s_sink = jnp.broadcast_to(sink.astype(jnp.float32).reshape(1, A_KV_HEADS, A_GROUP, 1, 1), s.shape[:-1] + (1,))
    p = jax.nn.softmax(jnp.concatenate([s, s_sink], axis=-1), axis=-1)[..., :n_ctx].astype(vc.dtype)
    o = jnp.einsum('bkgqs,bskd->bqkgd', p, vc)
    return o.reshape(bsz, n_ctx, A_HEADS * HEAD_DIM)


def diff_attend(q, keys, vals, lam):
    s = jnp.einsum('bqhmd,bshmd->bhmqs', q, keys).astype(jnp.float32) * (B_HEAD_DIM ** -0.5)
    p = jax.nn.softmax(s, axis=-1)
    w = p[:, :, 0] - lam * p[:, :, 1]
    return jnp.einsum('bhqs,bshe->bqhe', w.astype(vals.dtype), vals)


def diff_attn_latent(q, k, v, kc, vc, lam):
    bsz, n_tok = q.shape[:2]
    nb = n_tok // BLOCK
    keys = jnp.concatenate([k, kc], axis=1)
    vals = jnp.concatenate([v, vc], axis=1)
    qb = jnp.moveaxis(q.reshape(bsz, nb, BLOCK, B_HEADS, 2, B_HEAD_DIM), 1, 0)
    o = lax.map(lambda qblk: diff_attend(qblk, keys, vals, lam), qb)
    return jnp.moveaxis(o, 0, 1).reshape(bsz, n_tok, B_HEADS, 2 * B_HEAD_DIM)


def diff_head_norm(o, g, lam_init):
    bsz, n = o.shape[:2]
    return (rms_norm(o, g) * (1.0 - lam_init)).reshape(bsz, n, B_HEADS * 2 * B_HEAD_DIM)


def attn_mixer(xn, cn, w_in, w_out, sink, lam_vec, subln_g, lam_init, cos, sin, need_ctx):
    def proj(t):
        bsz, n, _ = t.shape
        aq, ak, av, bq, bk, bv = jnp.split(t @ w_in, ATTN_SPLITS, axis=-1)
        return (aq.reshape(bsz, n, A_HEADS, HEAD_DIM), ak.reshape(bsz, n, A_KV_HEADS, HEAD_DIM),
                av.reshape(bsz, n, A_KV_HEADS, HEAD_DIM), bq.reshape(bsz, n, B_HEADS, 2, B_HEAD_DIM),
                bk.reshape(bsz, n, B_HEADS, 2, B_HEAD_DIM), bv.reshape(bsz, n, B_HEADS, 2 * B_HEAD_DIM))

    aq, ak, av, bq, bk, bv = proj(xn)
    caq, cak, cav, cbq, cbk, cbv = proj(cn)
    aq, ak, bq, bk = (apply_rope(t, cos, sin) for t in (aq, ak, bq, bk))
    lv = lam_vec.astype(jnp.float32)
    lam = jnp.exp(jnp.sum(lv[0] * lv[1])) - jnp.exp(jnp.sum(lv[2] * lv[3])) + lam_init
    oa = window_gqa_latent(aq, ak, av, cak, cav, sink)
    ob = diff_head_norm(diff_attn_latent(bq, bk, bv, cbk, cbv, lam), subln_g, lam_init)
    y = jnp.concatenate([oa, ob], axis=-1) @ w_out
    yc = None
    if need_ctx:
        oca = gqa_context(caq, cak, cav, sink)
        ocb = diff_head_norm(diff_attend(cbq, cbk, cbv, lam), subln_g, lam_init)
        yc = jnp.concatenate([oca, ocb], axis=-1) @ w_out
    return y, yc


def gla_chunk_scan(q, k, v, loga, s0):
    bsz, n_tok = q.shape[:2]
    nc = n_tok // C_CHUNK

    def chunks(a):
        return a.reshape(bsz, nc, C_CHUNK, C_HEADS, a.shape[-1]).transpose(1, 0, 3, 2, 4).astype(jnp.float32)

    lower = jnp.tril(jnp.ones((C_CHUNK, C_CHUNK), dtype=bool))

    def step(state, inp):
        qc, kc, vc, ac = inp
        b = jnp.cumsum(ac, axis=2)
        rel = jnp.where(lower[:, :, None], b[:, :, :, None, :] - b[:, :, None, :, :], -jnp.inf)
        scores = jnp.einsum('bhtd,bhsd,bhtsd->bhts', qc, kc, jnp.exp(rel))
        out = (jnp.einsum('bhts,bhse->bhte', scores, vc)
               + jnp.einsum('bhtd,bhde->bhte', qc * jnp.exp(b), state))
        b_last = b[:, :, -1:, :]
        state = (state * jnp.exp(b_last[:, :, 0, :, None])
                 + jnp.einsum('bhsd,bhse->bhde', kc * jnp.exp(b_last - b), vc))
        return state, out

    _, o = lax.scan(step, s0, (chunks(q), chunks(k), chunks(v), chunks(loga)))
    return o.transpose(1, 0, 3, 2, 4).reshape(bsz, n_tok, C_HEADS, C_DV).astype(v.dtype)


def gla_final_state(k, v, loga):
    b = jnp.cumsum(loga.astype(jnp.float32), axis=1)
    w = jnp.exp(b[:, -1:] - b)
    return jnp.einsum('blhd,blhe->bhde', k.astype(jnp.float32) * w, v.astype(jnp.float32))


def gla_mixer(xn, cn, w_in, gate_w1, gate_w2, gate_b, norm_g, w_out, need_ctx):
    def proj(t):
        bsz, n, _ = t.shape
        q, k, v, g = jnp.split(t @ w_in, GLA_SPLITS, axis=-1)
        q = q.reshape(bsz, n, C_HEADS, C_DK) * (C_DK ** -0.5)
        k = k.reshape(bsz, n, C_HEADS, C_DK)
        v = v.reshape(bsz, n, C_HEADS, C_DV)
        loga = [(jax.nn.log_sigmoid(((t @ gate_w1[d]) @ gate_w2[d] + gate_b[d]).astype(jnp.float32))
                 / C_GATE_NORM).reshape(bsz, n, C_HEADS, C_DK) for d in range(2)]
        return q, k, v, g, loga

    def out_proj(o, g):
        bsz, n = o.shape[:2]
        return (rms_norm(o, norm_g).reshape(bsz, n, C_HEADS * C_DV) * jax.nn.silu(g)) @ w_out

    flip = lambda a: jnp.flip(a, axis=1)
    q, k, v, g, (la_f, la_b) = proj(xn)
    qc, kc, vc, gc, (lac_f, lac_b) = proj(cn)
    s_f = gla_final_state(kc, vc, lac_f)
    s_b = gla_final_state(flip(kc), flip(vc), flip(lac_b))
    o = (gla_chunk_scan(q, k, v, la_f, s_f)
         + flip(gla_chunk_scan(flip(q), flip(k), flip(v), flip(la_b), s_b)))
    y = out_proj(o, g)
    yc = None
    if need_ctx:
        z = jnp.zeros((cn.shape[0], C_HEADS, C_DK, C_DV), jnp.float32)
        oc = (gla_chunk_scan(qc, kc, vc, lac_f, z)
              + flip(gla_chunk_scan(flip(qc), flip(kc), flip(vc), flip(lac_b), z)))
        yc = out_proj(oc, gc)
    return y, yc


def conv_ffn(x, w_up, conv_w, conv_b, w_down):
    h = x @ w_up
    ch = h.shape[-1]
    h = lax.conv_general_dilated(h, conv_w[:, None, :].astype(h.dtype), window_strides=(1,),
                                 padding=((CONV_W // 2, CONV_W // 2),), dimension_numbers=('NWC', 'WIO', 'NWC'),
                                 feature_group_count=ch) + conv_b
    u, gt = jnp.split(h, 2, axis=-1)
    return (jax.nn.silu(gt) * u) @ w_down


def setup_inputs(seed: int = 0) -> dict:
    key = jax.random.key(seed)
    ks = jax.random.split(key, 24)
    nrm = lambda k, shape, s: jax.random.normal(k, shape, jnp.float32) * s
    D = D_MODEL
    return {
        'x': nrm(ks[0], (BATCH, SEQ, D), 1.0),
        'c': nrm(ks[1], (BATCH, D), 1.0),
        'ctx': nrm(ks[2], (BATCH, CTX_LEN, D), 1.0),
        'c_ctx': nrm(ks[3], (D,), 1.0),
        'mod_w': nrm(ks[4], (DEPTH, D, 6 * D), D ** -0.5),
        'mod_b': nrm(ks[5], (DEPTH, 6 * D), 0.02),
        'norm1_g': 1.0 + nrm(ks[6], (DEPTH, D), 0.02),
        'norm2_g': 1.0 + nrm(ks[7], (DEPTH, D), 0.02),
        'attn_w_in': nrm(ks[8], (N_EVEN, D, ATTN_IN), D ** -0.5),
        'attn_w_out': nrm(ks[9], (N_EVEN, ATTN_OUT, D), ATTN_OUT ** -0.5),
        'attn_sink': nrm(ks[10], (N_EVEN, A_HEADS), 0.5),
        'diff_lambda': nrm(ks[11], (N_EVEN, 4, B_HEAD_DIM), 0.1),
        'diff_subln_g': 1.0 + nrm(ks[12], (N_EVEN, 2 * B_HEAD_DIM), 0.02),
        'gla_w_in': nrm(ks[13], (N_ODD, D, GLA_IN), D ** -0.5),
        'gla_gate_w1': nrm(ks[14], (N_ODD, 2, D, C_GATE_RANK), D ** -0.5),
        'gla_gate_w2': nrm(ks[15], (N_ODD, 2, C_GATE_RANK, C_HEADS * C_DK), C_GATE_RANK ** -0.5),
        'gla_gate_b': nrm(ks[16], (N_ODD, 2, C_HEADS * C_DK), 0.1),
        'gla_norm_g': 1.0 + nrm(ks[17], (N_ODD, C_DV), 0.02),
        'gla_w_out': nrm(ks[18], (N_ODD, C_HEADS * C_DV, D), (C_HEADS * C_DV) ** -0.5),
        'ffn_w_up': nrm(ks[19], (DEPTH, D, 2 * D_FF), D ** -0.5),
        'ffn_conv_w': nrm(ks[20], (DEPTH, CONV_W, 2 * D_FF), CONV_W ** -0.5),
        'ffn_conv_b': nrm(ks[21], (DEPTH, 2 * D_FF), 0.02),
        'ffn_w_down': nrm(ks[22], (DEPTH, D_FF, D), D_FF ** -0.5),
        'final_norm_g': 1.0 + nrm(ks[23], (D,), 0.02),
    }


def reference(x, c, ctx, c_ctx, mod_w, mod_b, norm1_g, norm2_g, attn_w_in, attn_w_out, attn_sink,
              diff_lambda, diff_subln_g, gla_w_in, gla_gate_w1, gla_gate_w2, gla_gate_b, gla_norm_g,
              gla_w_out, ffn_w_up, ffn_conv_w, ffn_conv_b, ffn_w_down, final_norm_g):
    cos, sin = axial_rope_tables(x.shape[1])
    silu_c = jax.nn.silu(c)
    silu_cc = jax.nn.silu(c_ctx)
    h, hc = x, ctx
    for layer in range(DEPTH):
        need_ctx = layer < DEPTH - 1
        mod = (silu_c @ mod_w[layer] + mod_b[layer])[:, None, :]
        mod_c = (silu_cc @ mod_w[layer] + mod_b[layer])[None, None, :]
        sh1, sc1, g1, sh2, sc2, g2 = jnp.split(mod, 6, axis=-1)
        csh1, csc1, cg1, csh2, csc2, cg2 = jnp.split(mod_c, 6, axis=-1)
        xn = rms_norm(h, norm1_g[layer]) * (1.0 + sc1) + sh1
        cn = rms_norm(hc, norm1_g[layer]) * (1.0 + csc1) + csh1
        i = layer // 2
        if layer % 2 == 0:
            lam_init = 0.8 - 0.6 * math.exp(-B_LAMBDA_DECAY * layer)
            y, yc = attn_mixer(xn, cn, attn_w_in[i], attn_w_out[i], attn_sink[i], diff_lambda[i],
                               diff_subln_g[i], lam_init, cos, sin, need_ctx)
        else:
            y, yc = gla_mixer(xn, cn, gla_w_in[i], gla_gate_w1[i], gla_gate_w2[i], gla_gate_b[i],
                              gla_norm_g[i], gla_w_out[i], need_ctx)
        h = h + g1 * y
        h = h + g2 * conv_ffn(rms_norm(h, norm2_g[layer]) * (1.0 + sc2) + sh2, ffn_w_up[layer],
                              ffn_conv_w[layer], ffn_conv_b[layer], ffn_w_down[layer])
        if need_ctx:
            hc = hc + cg1 * yc
            hc = hc + cg2 * conv_ffn(rms_norm(hc, norm2_g[layer]) * (1.0 + csc2) + csh2, ffn_w_up[layer],
                                     ffn_conv_w[layer], ffn_conv_b[layer], ffn_w_down[layer])
    return rms_norm(h, final_norm_g)
```

```python
import numpy as np
from contextlib import ExitStack
import concourse.bass as bass
import concourse.mybir as mybir
from concourse.bass_utils import run_bass_kernel_spmd

F32 = mybir.dt.float32
BF16 = mybir.dt.bfloat16
ALU = mybir.AluOpType
AF = mybir.ActivationFunctionType
AX = mybir.AxisListType

D = 1024
T = 4096
TO = 2048
L = 256
NTOK = T + L
DFF = 2816
NJ = 22
EPS = 1e-6
LAM0 = 0.8 - 0.6 * 1.0
SCALE = 0.125

C_ID, C_PERM, C_BD, C_ELO, C_EHI, C_OLO, C_OHI, C_ONE, C_BAND = 0, 128, 256, 384, 512, 640, 768, 896, 1024
C_TRIF, C_SUFF, C_TRIR, C_PRER = 1664, 1792, 1920, 2048
NCOL = 2176

DEBUG = None


class Buf:
    __slots__ = ("name", "w", "r")

    def __init__(self, name=""):
        self.name = name
        self.w = None
        self.r = {}


class Eng:
    def __init__(self, name, handle, sem):
        self.name = name
        self.h = handle
        self.sem = sem
        self.cnt = 0
        self.seen = {}


class K:
    def __init__(self, nc, stack, ndma=24):
        self.nc = nc
        self.sems = {}
        self.engs = {}
        for nm, h in (("pe", nc.tensor), ("act", nc.scalar), ("dve", nc.vector),
                      ("pool", nc.gpsimd), ("sp", nc.sync)):
            s = stack.enter_context(nc.semaphore("s_" + nm))
            self.sems[nm] = s
            self.engs[nm] = Eng(nm, h, s)
        self.dsems = []
        for i in range(ndma):
            s = stack.enter_context(nc.semaphore("d%d" % i))
            self.sems["d%d" % i] = s
            self.dsems.append(["d%d" % i, 0])
        self.dnext = 0
        self.sems["cc"] = stack.enter_context(nc.semaphore("ccs"))
        self.cccnt = 0
        self.ninstr = 0

    def _wait(self, e, key, val):
        if e.seen.get(key, 0) >= val:
            return
        e.h.wait_ge(self.sems[key], val)
        e.seen[key] = val
        self.ninstr += 1

    def _deps(self, e, reads, writes):
        for b in reads:
            if b.w is not None:
                k, v, en = b.w
                if not (en == "pe" and e.name == "pe"):
                    self._wait(e, k, v)
        for b in writes:
            if b.w is not None:
                k, v, en = b.w
                if en != e.name:
                    self._wait(e, k, v)
            for k, (v, en) in b.r.items():
                if en != e.name:
                    self._wait(e, k, v)

    def _mark(self, key, val, en, reads, writes):
        for b in reads:
            b.r[key] = (val, en)
        for b in writes:
            b.w = (key, val, en)
            b.r = {}

    def op(self, en, fn, reads=(), writes=()):
        e = self.engs[en]
        self._deps(e, reads, writes)
        ins = fn(e.h)
        e.cnt += 1
        ins.then_inc(e.sem, 1)
        self._mark(en, e.cnt, en, reads, writes)
        self.ninstr += 1
        return ins

    def dma(self, qn, out, in_, reads=(), writes=(), **kw):
        e = self.engs[qn]
        self._deps(e, reads, writes)
        slot = self.dsems[self.dnext]
        self.dnext = (self.dnext + 1) % len(self.dsems)
        key = slot[0]
        if slot[1] > 0:
            self._wait(e, key, slot[1])
        slot[1] += 16
        e.h.dma_start(out=out, in_=in_, **kw).then_inc(self.sems[key], 16)
        self._mark(key, slot[1], "dma", reads, writes)
        self.ninstr += 1

    def allgather(self, groups, in_ap, out_ap, reads=(), writes=()):
        e = self.engs["pool"]
        self._deps(e, reads, writes)
        self.cccnt += 1
        e.h.collective_compute("AllGather", ALU.bypass, replica_groups=groups,
                               ins=[in_ap], outs=[out_ap]).then_inc(self.sems["cc"], 1)
        self._mark("cc", self.cccnt, "cc", reads, writes)
        self.ninstr += 1

    def barrier(self):
        for e in self.engs.values():
            for o in self.engs.values():
                if o is not e and o.cnt > 0:
                    self._wait(e, o.name, o.cnt)
            for key, c in self.dsems:
                if c > 0:
                    self._wait(e, key, c)
            if self.cccnt > 0:
                self._wait(e, "cc", self.cccnt)


def dap(t, off, dims):
    return bass.AP(t, off, [list(d) for d in dims])


class Prog:
    def __init__(self, debug=None):
        self.debug = debug
        self.nc = bass.Bass("TRN2", target_bir_lowering=False)
        self.outs = []

    def din(self, name, shape, dt=F32):
        return self.nc.dram_tensor(name, list(shape), dt, kind="ExternalInput")

    def dscr(self, name, shape, dt=F32, dbg=False):
        kind = "ExternalOutput" if dbg else "Internal"
        t = self.nc.dram_tensor(name, list(shape), dt, kind=kind)
        if dbg:
            self.outs.append(name)
        return t

    def build(self):
        nc = self.nc
        dbg = self.debug
        I = {}
        I["xin"] = self.din("xin", [NTOK, D])
        I["cvec"] = self.din("cvec", [128, 8, 2])
        I["mod_w"] = self.din("mod_w", [2, D, 6 * D])
        I["mod_b"] = self.din("mod_b", [2, 6 * D])
        I["mod_bfm"] = self.din("mod_bfm", [2, 128, 48])
        I["n1g"] = self.din("n1g", [2, 128, 8])
        I["n2g"] = self.din("n2g", [2, 128, 8])
        I["fng"] = self.din("fng", [D])
        I["w0"] = self.din("w0", [D, 2432])
        I["w0o"] = self.din("w0o", [D, D])
        I["sink"] = self.din("sink", [8])
        I["dlam"] = self.din("dlam", [256])
        I["subg"] = self.din("subg", [128, 1])
        I["ctab"] = self.din("ctab", [128, NTOK])
        I["stab"] = self.din("stab", [128, NTOK])
        I["consts"] = self.din("consts", [128, NCOL])
        I["sel"] = self.din("sel", [128, 2])
        I["w1"] = self.din("w1", [D, 3072])
        I["gw1"] = self.din("gw1", [128, 8, 32])
        I["gw2"] = self.din("gw2", [32, 2, 512])
        I["glng"] = self.din("glng", [128, 2])
        I["w1o"] = self.din("w1o", [D, D])
        I["wup"] = self.din("wup", [2, D, 2 * DFF])
        I["cw"] = self.din("cw", [2, 128, 44, 3])
        I["cbias"] = self.din("cbias", [2, 128, 44])
        I["wdn"] = self.din("wdn", [2, DFF, D])
        self.I = I
        S = {}
        S["qkt"] = self.dscr("qkt", [14, 128, NTOK], BF16, dbg == "b1")
        S["vs"] = self.dscr("vs", [128, 34, 1024], BF16, dbg == "b1")
        S["hm"] = self.dscr("hm", [TO + L, D], F32, dbg == "b3")
        S["h1"] = self.dscr("h1", [TO + L, D], F32, dbg == "c")
        S["hm1"] = self.dscr("hm1", [TO, D], F32, dbg == "e")
        S["row"] = self.dscr("row", [1, D], F32)
        S["rowg"] = self.dscr("rowg", [2, D], F32)
        S["row2"] = self.dscr("row2", [1, D], F32)
        S["rowg2"] = self.dscr("rowg2", [2, D], F32)
        S["st"] = self.dscr("st", [128, 1024], F32)
        S["ofwd"] = self.dscr("ofwd", [8, 128, TO], F32)
        S["oall1"] = self.dscr("oall1", [8, 128, TO], BF16)
        S["stg"] = self.dscr("stg", [256, 1024], F32)
        if dbg == "b1":
            S["dbgmod"] = self.dscr("dbgmod", [128, 96 + 32], F32, True)
        self.S = S
        self.y = nc.dram_tensor("y", [TO, D], F32, kind="ExternalOutput")
        self.outs.append("y")
        self.By = Buf("y")

        with ExitStack() as st:
            self.k = K(nc, st)
            self.st = st
            self.glob()
            self.layer0()
            if dbg in ("b1", "b3", "c"):
                self.finish()
                return nc
            self.layer1()
            self.finish()
        return nc

    def finish(self):
        k = self.k
        k.barrier()

    def sb(self, st, name, shape, dt=F32):
        self.uid = getattr(self, "uid", 0) + 1
        return st.enter_context(self.nc.sbuf_tensor("s%d_%s" % (self.uid, name), list(shape), dt))

    def ps(self, st, name, shape=(128, 512), dt=F32):
        self.uid = getattr(self, "uid", 0) + 1
        return st.enter_context(self.nc.psum_tensor("p%d_%s" % (self.uid, name), list(shape), dt))

    def glob(self):
        k, I, st = self.k, self.I, self.st
        self.cF = self.sb(st, "cF", [128, NCOL])
        self.cB = self.sb(st, "cB", [128, NCOL], BF16)
        self.BcF, self.BcB = Buf("cF"), Buf("cB")
        k.dma("sp", self.cF[:], I["consts"].ap(), writes=[self.BcF])
        k.op("dve", lambda e: e.tensor_copy(out=self.cB[:], in_=self.cF[:]), [self.BcF], [self.BcB])
        self.sel = self.sb(st, "sel", [128, 2])
        self.Bsel = Buf("sel")
        k.dma("sp", self.sel[:], I["sel"].ap(), writes=[self.Bsel])
        self.silu = self.sb(st, "silu", [128, 8, 2])
        self.Bsilu = Buf("silu")
        cv = self.sb(st, "cv", [128, 8, 2])
        Bcv = Buf()
        k.dma("sp", cv[:], I["cvec"].ap(), writes=[Bcv])
        k.op("act", lambda e: e.activation(out=self.silu[:], in_=cv[:], func=AF.Silu), [Bcv], [self.Bsilu])
        self.modfm = self.sb(st, "modfm", [128, 48, 2])
        self.Bmodfm = Buf("modfm")
        self.ab = self.sb(st, "ab", [128, 2, 4, 8])
        self.Bab = Buf("ab")
        self.gbc = self.sb(st, "gbc", [128, 2, 2, D])
        self.Bgbc = Buf("gbc")
        self.epsT = self.sb(st, "epsT", [128, 1])
        self.Beps = Buf("eps")
        k.op("pool", lambda e: e.memset(self.epsT[:], EPS), [], [self.Beps])
        self.stat = self.sb(st, "stat", [128, 14, 9])
        self.Bstat = Buf("stat")

    def compute_mod(self, l, need_ctx):
        k, I = self.k, self.I
        nc = self.nc
        with ExitStack() as st:
            mw = [self.sb(st, "mw%d" % i, [128, 8, 512]) for i in range(2)]
            self.srep = self.sb(st, "srep", [128, 2, 8, 128])
            self.Bsrep = Buf("srep")
            for v in range(2):
                for kk in range(8):
                    k.op("act", lambda e: e.activation(out=self.srep[:, v, kk, :], in_=self.cF[:, C_ONE:C_ONE + 128],
                                                       func=AF.Copy, scale=self.silu[:, kk, v:v + 1]),
                         [self.Bsilu, self.BcF], [self.Bsrep])
            Bmw = [Buf(), Buf()]
            pm = self.ps(st, "pm", [128, 512])
            Bpm = Buf()
            pg = [self.ps(st, "pg%d" % i, [128, 512]) for i in range(2)]
            Bpg = [Buf(), Buf()]
            mbb = self.sb(st, "mbb", [128, 512])
            Bmbb = Buf()
            mbf = self.sb(st, "mbf", [128, 48])
            Bmbf = Buf()
            ng = self.sb(st, "ng", [128, 2, 8])
            Bng = Buf()
            k.dma("sp", mbf[:], I["mod_bfm"].ap()[l], writes=[Bmbf])
            k.dma("sp", ng[:, 0, :], I["n1g"].ap()[l], writes=[Bng])
            k.dma("sp", ng[:, 1, :], I["n2g"].ap()[l], writes=[Bng])
            it = 0
            for cb in range(12):
                j = it % 2
                it += 1
                src = I["mod_w"].ap()[l, :, cb * 512:(cb + 1) * 512].rearrange("(k p) n -> p k n", p=128)
                k.dma("sp", mw[j][:], src, writes=[Bmw[j]])
                if cb in (4, 5, 10, 11):
                    gi = 0 if cb < 6 else 1
                    half = cb % 2 if cb < 6 else (cb - 10)
                    k.dma("sp", mbb[:], dap(I["mod_b"], l * 6 * D + cb * 512, [[0, 128], [1, 512]]), writes=[Bmbb])
                    for v in range(2 if need_ctx else 1):
                        p = pg[v]
                        for kk in range(8):
                            k.op("pe", lambda e: e.matmul(p[:], lhsT=self.srep[:, v, kk, :], rhs=mw[j][:, kk, :],
                                                          start=(kk == 0), stop=(kk == 7)),
                                 [self.Bsrep, Bmw[j]], [Bpg[v]])
                        k.op("dve", lambda e: e.tensor_tensor(out=self.gbc[:, v, gi, half * 512:(half + 1) * 512],
                                                              in0=p[:], in1=mbb[:], op=ALU.add),
                             [Bpg[v], Bmbb], [self.Bgbc])
                else:
                    for jj in range(4):
                        ch = cb * 4 + jj
                        for kk in range(8):
                            k.op("pe", lambda e: e.matmul(pm[:, ch * 2:ch * 2 + 2], lhsT=mw[j][:, kk, jj * 128:(jj + 1) * 128],
                                                          rhs=self.silu[:, kk, :], start=(kk == 0), stop=(kk == 7)),
                                 [self.Bsilu, Bmw[j]], [Bpm])
            for v in range(2):
                k.op("dve", lambda e: e.tensor_tensor(out=self.modfm[:, :, v],
                                                      in0=pm[:, 0:96].rearrange("p (c v) -> p c v", v=2)[:, :, v],
                                                      in1=mbf[:], op=ALU.add), [Bpm, Bmbf], [self.Bmodfm])
            for v in range(2):
                for (ai, sc0, sh0, gi) in ((0, 8, 0, 0), (2, 32, 24, 1)):
                    k.op("dve", lambda e: e.scalar_tensor_tensor(out=self.ab[:, v, ai, :], in0=self.modfm[:, sc0:sc0 + 8, v],
                                                                 scalar=1.0, in1=ng[:, gi, :], op0=ALU.add, op1=ALU.mult),
                         [self.Bmodfm, Bng], [self.Bab])
                    k.op("dve", lambda e: e.tensor_copy(out=self.ab[:, v, ai + 1, :], in_=self.modfm[:, sh0:sh0 + 8, v]),
                         [self.Bmodfm], [self.Bab])
            k.barrier()

    def norm_a(self, R, srcs, v, which, X, BX, col0, rd=()):
        k = self.k
        i = R["i"] % R["n"]
        ip = R["i"] % R["npt"]
        R["i"] += 1
        xt, Bxt = R["xt"][i], R["Bxt"][i]
        nrows = 0
        for ap, r0 in srcs:
            n = ap.shape[0]
            k.dma("sp", xt[r0:r0 + n, :], ap, reads=list(rd), writes=[Bxt])
            nrows = max(nrows, r0 + n)
        ss, Bss = R["ss"][i], R["Bss"][i]
        junk, Bjunk = R["junk"], R["Bjunk"]
        k.op(R.get("ms_eng", "pool"), lambda e: e.memset(ss[:], 0.0), [], [Bss])
        k.op("act", lambda e: e.activation(out=junk[:nrows, :], in_=xt[:nrows, :], func=AF.Square,
                                           accum_out=ss[:nrows, 0:1]), [Bxt, Bss], [Bjunk, Bss])
        k.op("act", lambda e: e.activation(out=ss[:nrows, 1:2], in_=ss[:nrows, 0:1], func=AF.Ln, scale=1.0 / D,
                                           bias=self.epsT[:nrows, 0:1]), [Bss, self.Beps], [Bss])
        k.op("act", lambda e: e.activation(out=ss[:nrows, 2:3], in_=ss[:nrows, 1:2], func=AF.Exp, scale=-0.5), [Bss], [Bss])
        xh, Bxh = R["xh"][i], R["Bxh"][i]
        k.op("act", lambda e: e.activation(out=xh[:nrows, :], in_=xt[:nrows, :], func=AF.Copy, scale=ss[:nrows, 2:3]),
             [Bxt, Bss], [Bxh])
        return (i, nrows, v, which, X, BX, col0, ip)

    def norm_b(self, R, cx):
        k = self.k
        i, nrows, v, which, X, BX, col0, ip = cx
        xt, Bxt = R["xt"][i], R["Bxt"][i]
        xh, Bxh = R["xh"][i], R["Bxh"][i]
        pt, Bpt = R["pt"][ip], R["Bpt"][ip]
        for kk in range(8):
            k.op("pe", lambda e: e.transpose(out=pt[:, kk * 128:kk * 128 + nrows], in_=xh[:nrows, kk * 128:(kk + 1) * 128],
                                             identity=self.cB[:nrows, C_ID:C_ID + nrows]), [Bxh, self.BcB], [Bpt])
        a_bc = self.ab[:, v, 2 * which, :].unsqueeze(2).to_broadcast([128, 8, nrows])
        b_bc = self.ab[:, v, 2 * which + 1, :].unsqueeze(2).to_broadcast([128, 8, nrows])
        ptv = pt[:].rearrange("p (k t) -> p k t", t=128)[:, :, :nrows]
        tmpv = xt[:].rearrange("p (k t) -> p k t", t=128)[:, :, :nrows]
        k.op("dve", lambda e: e.tensor_tensor(out=tmpv, in0=ptv, in1=a_bc, op=ALU.mult), [Bpt, self.Bab, Bxh], [Bxt])
        k.op("dve", lambda e: e.tensor_tensor(out=X[:, :, col0:col0 + nrows], in0=tmpv, in1=b_bc, op=ALU.add), [Bxt, self.Bab], [BX])

    def norm_block(self, R, srcs, v, which, X, BX, col0, rd=()):
        self.norm_b(R, self.norm_a(R, srcs, v, which, X, BX, col0, rd))

    def norm_seq(self, R, items):
        prev = None
        for it in items:
            cx = self.norm_a(R, *it)
            if prev is not None:
                self.norm_b(R, prev)
            prev = cx
        if prev is not None:
            self.norm_b(R, prev)

    def norm_res(self, st, nbuf=2, npt=2):
        R = {"i": 0, "n": nbuf, "npt": npt}
        R["xt"] = [self.sb(st, "nb_xt%d" % i, [128, D]) for i in range(nbuf)]
        R["Bxt"] = [Buf() for _ in range(nbuf)]
        R["ss"] = [self.sb(st, "nb_ss%d" % i, [128, 4]) for i in range(nbuf)]
        R["Bss"] = [Buf() for _ in range(nbuf)]
        R["junk"] = self.sb(st, "nb_junk", [128, D], BF16)
        R["Bjunk"] = Buf()
        R["xh"] = [self.sb(st, "nb_xh%d" % i, [128, D], BF16) for i in range(nbuf)]
        R["Bxh"] = [Buf() for _ in range(nbuf)]
        R["pt"] = [self.ps(st, "nb_pt%d" % i, [128, D], BF16) for i in range(npt)]
        R["Bpt"] = [Buf() for _ in range(npt)]
        return R

    def layer0(self):
        k, I = self.k, self.I
        sw = ExitStack()
        W = self.sb(sw, "b1_W", [128, 8, 2432], BF16)
        BW = Buf()
        for g0 in range(0, 2432, 512):
            g1 = min(2432, g0 + 512)
            k.dma("pool", W[:, :, g0:g1], I["w0"].ap()[:, g0:g1].rearrange("(k p) n -> p k n", p=128), writes=[BW])
        self.compute_mod(0, True)
        if self.debug == "b1":
            k = self.k
            k.dma("sp", self.S["dbgmod"].ap()[:, 0:96], self.modfm[:].rearrange("p c v -> p (c v)"), reads=[self.Bmodfm])
            k.dma("sp", self.S["dbgmod"].ap()[:, 96:128], self.ab[:, 0, :, :].rearrange("p a k -> p (a k)"), reads=[self.Bab])
        self.phase_b1(W, BW)
        sw.close()
        if self.debug == "b1":
            return
        self.phase_b2()
        if self.debug == "b3":
            return
        self.ffn(0)

    def phase_b1(self, W, BW):
        k, I, S, nc = self.k, self.I, self.S, self.nc
        with ExitStack() as st:
            R = self.norm_res(st, nbuf=3, npt=2)
            R["ms_eng"] = "dve"
            X = [self.sb(st, "b1_X%d" % i, [128, 8, 512], BF16) for i in range(2)]
            BX = [Buf(), Buf()]
            pp = [self.ps(st, "b1_pp%d" % i) for i in range(2)]
            Bpp = [Buf(), Buf()]
            pr = self.ps(st, "b1_pr")
            Bpr = Buf()
            pq = self.ps(st, "b1_pq")
            Bpq = Buf()
            pv = [self.ps(st, "b1_pv%d" % i) for i in range(2)]
            Bpv = [Buf(), Buf()]
            tab = [self.sb(st, "b1_tab%d" % i, [128, 2, 512]) for i in range(2)]
            Btab = [Buf(), Buf()]
            qsb = [self.sb(st, "b1_qsb%d" % i, [128, 512], BF16) for i in range(2)]
            Bqsb = [Buf(), Buf()]
            sq = [self.sb(st, "b1_sq%d" % i, [128, 512], BF16) for i in range(2)]
            Bsq = [Buf(), Buf()]
            t1 = [self.sb(st, "b1_t1%d" % i, [128, 512]) for i in range(2)]
            Bt1 = [Buf(), Buf()]
            t2 = [self.sb(st, "b1_t2%d" % i, [128, 512]) for i in range(2)]
            Bt2 = [Buf(), Buf()]
            ro = [self.sb(st, "b1_ro%d" % i, [128, 512], BF16) for i in range(3)]
            Bro = [Buf(), Buf(), Buf()]
            vo = [self.sb(st, "b1_vo%d" % i, [128, 1024], BF16) for i in range(2)]
            Bvo = [Buf(), Buf()]
            k.op("pool", lambda e: e.memset(self.stat[:], 0.0), [], [self.Bstat])
            for j in range(2):
                k.op("pool", lambda e: e.memset(vo[j][:], 0.0), [], [Bvo[j]])
            Bqkt, Bvs = Buf("qkt"), Buf("vs")
            self.Bqkt, self.Bvs = Bqkt, Bvs
            cnt = 0
            vcnt = 0
            for ti in range(9):
                n = 512 if ti < 8 else 256
                c0 = ti * 512
                xi = ti % 2
                v = 0 if ti < 8 else 1
                self.norm_seq(R, [([(I["xin"].ap()[c0 + b * 128:c0 + (b + 1) * 128, :], 0)], v, 0, X[xi], BX[xi], b * 128)
                                  for b in range(n // 128)])
                k.dma("sp", tab[xi][:, 0, :n], I["ctab"].ap()[:, c0:c0 + n], writes=[Btab[xi]])
                k.dma("sp", tab[xi][:, 1, :n], I["stab"].ap()[:, c0:c0 + n], writes=[Btab[xi]])
                if ti < 4 or ti == 8:
                    chunks = list(range(14))
                elif ti == 4:
                    chunks = [4, 5, 10, 11, 12, 13]
                else:
                    chunks = [10, 11, 12, 13]
                def st_a(ci_):
                    ch = chunks[ci_]
                    j = (cnt + ci_) % 2
                    p = pp[j]
                    for kk in range(8):
                        k.op("pe", lambda e: e.matmul(p[:, :n], lhsT=W[:, kk, ch * 128:(ch + 1) * 128], rhs=X[xi][:, kk, :n],
                                                      start=(kk == 0), stop=(kk == 7)), [BW, BX[xi]], [Bpp[j]])
                    k.op("act", lambda e: e.activation(out=sq[j][:, :n], in_=p[:, :n], func=AF.Square), [Bpp[j]], [Bsq[j]])
                    k.op("act", lambda e: e.activation(out=qsb[j][:, :n], in_=p[:, :n], func=AF.Copy), [Bpp[j]], [Bqsb[j]])

                def st_b(ci_):
                    ch = chunks[ci_]
                    j = (cnt + ci_) % 2
                    j3 = (cnt + ci_) % 3
                    p = pp[j]
                    k.op("pe", lambda e: e.matmul(pq[:, :n], lhsT=self.cB[:, C_BD:C_BD + 128], rhs=sq[j][:, :n],
                                                  start=True, stop=True), [self.BcB, Bsq[j]], [Bpq])
                    k.op("dve", lambda e: e.tensor_reduce(out=self.stat[:, ch, ti:ti + 1], in_=pq[:, :n], axis=AX.X, op=ALU.max),
                         [Bpq], [self.Bstat])
                    k.op("pe", lambda e: e.matmul(pr[:, :n], lhsT=self.cB[:, C_PERM:C_PERM + 128], rhs=qsb[j][:, :n],
                                                  start=True, stop=True), [self.BcB, Bqsb[j]], [Bpr])
                    k.op("dve", lambda e: e.tensor_tensor(out=t1[j][:, :n], in0=p[:, :n], in1=tab[xi][:, 0, :n], op=ALU.mult),
                         [Bpp[j], Btab[xi]], [Bt1[j]])
                    k.op("dve", lambda e: e.tensor_tensor(out=t2[j][:, :n], in0=pr[:, :n], in1=tab[xi][:, 1, :n], op=ALU.mult),
                         [Bpr, Btab[xi]], [Bt2[j]])
                    k.op("pool", lambda e: e.tensor_tensor(out=ro[j3][:, :n], in0=t1[j][:, :n], in1=t2[j][:, :n], op=ALU.add),
                         [Bt1[j], Bt2[j]], [Bro[j3]])
                    k.dma("sp", S["qkt"].ap()[ch, :, c0:c0 + n], ro[j3][:, :n], reads=[Bro[j3]], writes=[Bqkt])

                for ci_ in range(len(chunks) + 1):
                    if ci_ < len(chunks):
                        st_a(ci_)
                    if ci_ >= 1:
                        st_b(ci_ - 1)
                cnt += len(chunks)
                for b in range(n // 128):
                    j = vcnt % 2
                    vcnt += 1
                    blk = (c0 + b * 128) // 128
                    for kk in range(8):
                        k.op("pe", lambda e: e.matmul(pv[0][:, :], lhsT=X[xi][:, kk, b * 128:(b + 1) * 128], rhs=W[:, kk, 1792:2304],
                                                      start=(kk == 0), stop=(kk == 7)), [BW, BX[xi]], [Bpv[0]])
                    for kk in range(8):
                        k.op("pe", lambda e: e.matmul(pv[1][:, :128], lhsT=X[xi][:, kk, b * 128:(b + 1) * 128], rhs=W[:, kk, 2304:2432],
                                                      start=(kk == 0), stop=(kk == 7)), [BW, BX[xi]], [Bpv[1]])
                    k.op("act", lambda e: e.activation(out=vo[j][:, 0:512], in_=pv[0][:, :], func=AF.Copy), [Bpv[0]], [Bvo[j]])
                    k.op("dve", lambda e: e.tensor_copy(out=vo[j][:, 512:1024].rearrange("p (a c) -> p a c", c=256)[:, :, 0:64],
                                                        in_=pv[1][:, 0:128].rearrange("p (a c) -> p a c", c=64)), [Bpv[1]], [Bvo[j]])
                    k.op("dve", lambda e: e.tensor_copy(out=vo[j][:, 512:1024].rearrange("p (a c) -> p a c", c=256)[:, :, 192:256],
                                                        in_=pv[1][:, 0:128].rearrange("p (a c) -> p a c", c=64)), [Bpv[1]], [Bvo[j]])
                    k.dma("sp", S["vs"].ap()[:, blk, :], vo[j][:], reads=[Bvo[j]], writes=[Bvs])
            k.barrier()

    def phase_b2(self):
        k, I, S, nc = self.k, self.I, self.S, self.nc
        cB, cF = self.cB, self.cF
        NQ = TO + L
        with ExitStack() as so:
            Oall = self.sb(so, "Oall", [128, 8, NQ], BF16)
            BO = Buf("Oall")
            wo_pre = self.load_wo(so, I["w0o"], defer=True)
            sm = self.sb(so, "sm", [128, 2, 14])
            negm = self.sb(so, "negm", [128, 2, 2, 4])
            skv = self.sb(so, "skv", [128, 4])
            lamv = self.sb(so, "lamv", [128, 4])
            sgs = self.sb(so, "sgs", [128, 1])
            Bsm, Bnegm, Bskv, Blam, Bsgs = Buf(), Buf(), Buf(), Buf(), Buf()
            with ExitStack() as st:
                smax = self.sb(st, "smax", [128, 14])
                Bsmax = Buf()
                pst = self.ps(st, "pst")
                Bpst = Buf()
                k.op("dve", lambda e: e.tensor_reduce(out=smax[:], in_=self.stat[:], axis=AX.X, op=ALU.max), [self.Bstat], [Bsmax])
                k.op("pe", lambda e: e.matmul(pst[:, 0:14], lhsT=cF[:, C_ELO:C_ELO + 128], rhs=smax[:], start=True, stop=True),
                     [self.BcF, Bsmax], [Bpst])
                k.op("pe", lambda e: e.matmul(pst[:, 14:28], lhsT=cF[:, C_EHI:C_EHI + 128], rhs=smax[:], start=True, stop=True),
                     [self.BcF, Bsmax], [Bpst])
                k.op("dve", lambda e: e.tensor_copy(out=sm[:].rearrange("p a c -> p (a c)"), in_=pst[:, 0:28]), [Bpst], [Bsm])
                m2 = self.sb(st, "m2", [128, 2, 2, 4])
                Bm2 = Buf()
                for j in range(2):
                    for kv in range(2):
                        k.op("dve", lambda e: e.tensor_scalar(out=m2[:, 0, j, 2 * kv:2 * kv + 2], in0=sm[:, j, 2 * kv:2 * kv + 2],
                                                              scalar1=sm[:, j, 4 + kv:5 + kv], scalar2=None, op0=ALU.mult), [Bsm], [Bm2])
                    k.op("dve", lambda e: e.tensor_tensor(out=m2[:, 1, j, :], in0=sm[:, j, 6:10], in1=sm[:, j, 10:14], op=ALU.mult),
                         [Bsm], [Bm2])
                lnm = self.sb(st, "lnm", [128, 16])
                Blnm = Buf()
                k.op("act", lambda e: e.activation(out=lnm[:], in_=m2[:].rearrange("p a b c -> p (a b c)"), func=AF.Ln), [Bm2], [Blnm])
                k.op("act", lambda e: e.activation(out=lnm[:], in_=lnm[:], func=AF.Exp, scale=0.5), [Blnm], [Blnm])
                k.op("dve", lambda e: e.tensor_scalar(out=negm[:].rearrange("p a b c -> p (a b c)"), in0=lnm[:], scalar1=-SCALE,
                                                      scalar2=None, op0=ALU.mult), [Blnm], [Bnegm])
                sk = self.sb(st, "sk", [128, 8])
                Bsk = Buf()
                k.dma("sp", sk[:], dap(I["sink"], 0, [[0, 128], [1, 8]]), writes=[Bsk])
                esk = self.sb(st, "esk", [128, 2, 4])
                Besk = Buf()
                k.op("dve", lambda e: e.tensor_tensor(out=esk[:], in0=sk[:].rearrange("p (c j) -> p j c", j=2), in1=negm[:, 0, :, :],
                                                      op=ALU.add), [Bsk, Bnegm], [Besk])
                k.op("act", lambda e: e.activation(out=esk[:], in_=esk[:], func=AF.Exp), [Besk], [Besk])
                k.op("dve", lambda e: e.tensor_copy(out=skv[0:64, :], in_=esk[0:64, 0, :]), [Besk], [Bskv])
                k.op("dve", lambda e: e.tensor_copy(out=skv[64:128, :], in_=esk[64:128, 1, :]), [Besk], [Bskv])
                dl = self.sb(st, "dl", [128, 256])
                Bdl = Buf()
                k.dma("sp", dl[:], dap(I["dlam"], 0, [[0, 128], [1, 256]]), writes=[Bdl])
                dp = self.sb(st, "dp", [128, 2, 64])
                Bdp = Buf()
                dlv = dl[:].rearrange("p (a b d) -> p a b d", a=2, b=2)
                k.op("dve", lambda e: e.tensor_tensor(out=dp[:], in0=dlv[:, :, 0, :], in1=dlv[:, :, 1, :], op=ALU.mult), [Bdl], [Bdp])
                k.op("dve", lambda e: e.tensor_reduce(out=lamv[:, 0:2], in_=dp[:], axis=AX.X, op=ALU.add), [Bdp], [Blam])
                k.op("act", lambda e: e.activation(out=lamv[:, 0:2], in_=lamv[:, 0:2], func=AF.Exp), [Blam], [Blam])
                k.op("dve", lambda e: e.tensor_tensor(out=lamv[:, 2:3], in0=lamv[:, 1:2], in1=lamv[:, 0:1], op=ALU.subtract), [Blam], [Blam])
                k.op("dve", lambda e: e.tensor_scalar(out=lamv[:, 3:4], in0=lamv[:, 2:3], scalar1=-LAM0, scalar2=None, op0=ALU.add),
                     [Blam], [Blam])
                sg0 = self.sb(st, "sg0", [128, 1])
                Bsg0 = Buf()
                k.dma("sp", sg0[:], I["subg"].ap(), writes=[Bsg0])
                k.op("dve", lambda e: e.tensor_scalar(out=sgs[:], in0=sg0[:], scalar1=(1.0 - LAM0), scalar2=None, op0=ALU.mult),
                     [Bsg0], [Bsgs])
                k.barrier()
            with ExitStack() as st:
                kaT = self.sb(st, "kaT", [128, 2, 2432], BF16)
                qaT = self.sb(st, "qaT", [128, 4, NQ], BF16)
                vaT = self.sb(st, "vaT", [128, 19, 512], BF16)
                Bka, Bqa, Bva = Buf(), Buf(), Buf()
                for kv in range(2):
                    k.dma("sp", kaT[:, kv, 0:2176], S["qkt"].ap()[4 + kv, :, 0:2176], reads=[self.Bqkt], writes=[Bka])
                    k.dma("sp", kaT[:, kv, 2176:2432], S["qkt"].ap()[4 + kv, :, T:NTOK], reads=[self.Bqkt], writes=[Bka])
                for c in range(4):
                    k.dma("sp", qaT[:, c, 0:TO], S["qkt"].ap()[c, :, 0:TO], reads=[self.Bqkt], writes=[Bqa])
                    k.dma("sp", qaT[:, c, TO:NQ], S["qkt"].ap()[c, :, T:NTOK], reads=[self.Bqkt], writes=[Bqa])
                k.dma("sp", vaT[:, 0:17, :], S["vs"].ap()[:, 0:17, 512:1024], reads=[self.Bvs], writes=[Bva])
                k.dma("sp", vaT[:, 17:19, :], S["vs"].ap()[:, 32:34, 512:1024], reads=[self.Bvs], writes=[Bva])
                pss = [self.ps(st, "a_ps%d" % i) for i in range(4)]
                Bpss = [Buf() for _ in range(4)]
                po = self.ps(st, "a_po")
                pd = self.ps(st, "a_pd")
                Bpo, Bpd = Buf(), Buf()
                ptl = [self.sb(st, "a_pt%d" % i, [128, 512], BF16) for i in range(4)]
                Bptl = [Buf() for _ in range(4)]
                den = self.sb(st, "a_den", [128, 512])
                Bden = Buf()
                cnt = 0
                for qi in range(5):
                    nq = 512 if qi < 4 else 256
                    qc0 = qi * 512
                    chunks = [(2176 + 128 * cc, 17 + cc, 0, nq, None) for cc in range(2)]
                    if qi < 4:
                        spans = [(0, 128, 384), (0, 256, 256), (0, 384, 128), (128, 384, 128), (256, 256, 128), (384, 128, 128)]
                        for j6 in range(6):
                            kc = qc0 - 128 + 128 * j6
                            if kc < 0:
                                continue
                            s0, ln, u0 = spans[j6]
                            chunks.append((kc, kc // 128, s0, ln, u0))
                    for c in range(4):
                        kv = c // 2
                        steps = [(j, ch) for j in range(2) for ch in chunks]
                        nst = len(steps)
                        base = cnt
                        cnt += nst

                        def qk_step(si):
                            j, (kcol, vblk, s0, ln, u0) = steps[si]
                            i3 = (base + si) % 4
                            r0 = 64 * j
                            k.op("pe", lambda e: e.matmul(pss[i3][:, :ln], lhsT=kaT[r0:r0 + 64, kv, kcol:kcol + 128],
                                                          rhs=qaT[r0:r0 + 64, c, qc0 + s0:qc0 + s0 + ln], start=True, stop=True),
                                 [Bka, Bqa], [Bpss[i3]])
                            k.op("act", lambda e: e.activation(out=ptl[i3][:, :ln], in_=pss[i3][:, :ln], func=AF.Exp, scale=SCALE,
                                                               bias=negm[:, 0, j, c:c + 1]), [Bpss[i3], Bnegm], [Bptl[i3]])
                            if u0 is not None:
                                k.op("dve", lambda e: e.tensor_tensor(out=ptl[i3][:, :ln], in0=ptl[i3][:, :ln],
                                                                      in1=cB[:, C_BAND + u0:C_BAND + u0 + ln], op=ALU.mult),
                                     [Bptl[i3], self.BcB], [Bptl[i3]])

                        def pv_step(si):
                            j, (kcol, vblk, s0, ln, u0) = steps[si]
                            i3 = (base + si) % 4
                            first = (si == 0)
                            vc = (2 * kv + j) * 128
                            k.op("pe", lambda e: e.matmul(po[:, s0:s0 + ln], lhsT=vaT[:, vblk, vc:vc + 128], rhs=ptl[i3][:, :ln],
                                                          start=first, stop=False, skip_group_check=True), [Bva, Bptl[i3]], [Bpo])
                            oc = C_OLO if j == 0 else C_OHI
                            k.op("pe", lambda e: e.matmul(pd[:, s0:s0 + ln], lhsT=cB[:, oc:oc + 128], rhs=ptl[i3][:, :ln],
                                                          start=first, stop=False, skip_group_check=True), [self.BcB, Bptl[i3]], [Bpd])
                        for t2 in range(0, nst + 2, 2):
                            for si in (t2, t2 + 1):
                                if si < nst:
                                    qk_step(si)
                            for si in (t2 - 2, t2 - 1):
                                if 0 <= si < nst:
                                    pv_step(si)
                        k.op("act", lambda e: e.activation(out=den[:, :nq], in_=pd[:, :nq], func=AF.Ln, bias=skv[:, c:c + 1]),
                             [Bpd, Bskv], [Bden])
                        k.op("act", lambda e: e.activation(out=den[:, :nq], in_=den[:, :nq], func=AF.Exp, scale=-1.0), [Bden], [Bden])
                        k.op("dve", lambda e: e.tensor_tensor(out=Oall[:, c, qc0:qc0 + nq], in0=po[:, :nq], in1=den[:, :nq], op=ALU.mult),
                             [Bpo, Bden], [BO])
                k.barrier()
            self._wo_issue()
            with ExitStack() as st:
                kbT = [self.sb(st, "kbT%d" % i, [128, NTOK], BF16) for i in range(2)]
                qbT = [self.sb(st, "qbT%d" % i, [128, NQ], BF16) for i in range(2)]
                vb = [self.sb(st, "vb%d" % i, [128, 34, 128], BF16) for i in range(2)]
                Bkb, Bqb, Bvb = [Buf(), Buf()], [Buf(), Buf()], [Buf(), Buf()]
                pss = [self.ps(st, "b_ps%d" % i, [128, 1024]) for i in range(2)]
                Bpss = [Buf() for _ in range(2)]
                po = [self.ps(st, "b_po%d" % i) for i in range(2)]
                Bpo = [Buf(), Buf()]
                pf = [self.ps(st, "b_pf%d" % i) for i in range(2)]
                Bpf = [Buf(), Buf()]
                ptl = [self.sb(st, "b_pt%d" % i, [128, 1024], BF16) for i in range(3)]
                Bptl = [Buf() for _ in range(3)]
                acc2 = self.sb(st, "b_acc", [128, 1024])
                Bacc2 = Buf()
                acc = [acc2[:, 0:512], acc2[:, 512:1024]]
                Bacc = [Bacc2, Bacc2]
                r = [self.sb(st, "b_r%d" % i, [128, 512]) for i in range(2)]
                Br = [Buf(), Buf()]
                tt = [self.sb(st, "b_t%d" % i, [128, 512]) for i in range(2)]
                Btt = [Buf(), Buf()]
                od = self.sb(st, "b_od", [128, 512])
                Bod = Buf()
                sqd = self.sb(st, "b_sqd", [128, 512], BF16)
                Bsqd = Buf()
                negmB = self.sb(st, "b_negm", [128, 4])
                BnegmB = Buf()
                k.op("dve", lambda e: e.tensor_tensor(out=negmB[:], in0=negm[:, 1, 0, :], in1=negm[:, 1, 1, :], op=ALU.min), [Bnegm], [BnegmB])
                cnt = 0
                for h in range(4):
                    hb = h % 2
                    k.dma("sp", kbT[hb][:], S["qkt"].ap()[10 + h], reads=[self.Bqkt], writes=[Bkb[hb]])
                    k.dma("sp", qbT[hb][:, 0:TO], S["qkt"].ap()[6 + h, :, 0:TO], reads=[self.Bqkt], writes=[Bqb[hb]])
                    k.dma("sp", qbT[hb][:, TO:NQ], S["qkt"].ap()[6 + h, :, T:NTOK], reads=[self.Bqkt], writes=[Bqb[hb]])
                    k.dma("sp", vb[hb][:], S["vs"].ap()[:, :, h * 128:(h + 1) * 128], reads=[self.Bvs], writes=[Bvb[hb]])
                    for qi in range(5):
                        nq = 512 if qi < 4 else 256
                        qc0 = qi * 512
                        chunks = list(range(34)) if qi < 4 else [32, 33]
                        nch = len(chunks)
                        base = cnt
                        cnt += nch

                        def qk_pair(ci):
                            kc = chunks[ci]
                            i2 = (base + ci) % 2
                            i3 = (base + ci) % 3
                            for m in range(2):
                                r0 = 64 * m
                                k.op("pe", lambda e: e.matmul(pss[i2][:, m * 512:m * 512 + nq], lhsT=kbT[hb][r0:r0 + 64, kc * 128:(kc + 1) * 128],
                                                              rhs=qbT[hb][r0:r0 + 64, qc0:qc0 + nq], start=True, stop=True),
                                     [Bkb[hb], Bqb[hb]], [Bpss[i2]])
                            k.op("act", lambda e: e.activation(out=ptl[i3][:].rearrange("p (m n) -> p m n", m=2)[:, :, :nq],
                                                               in_=pss[i2][:].rearrange("p (m n) -> p m n", m=2)[:, :, :nq],
                                                               func=AF.Exp, scale=SCALE, bias=negmB[:, h:h + 1]), [Bpss[i2], BnegmB], [Bptl[i3]])

                        def pv_pair(ci):
                            kc = chunks[ci]
                            i3 = (base + ci) % 3
                            for m in range(2):
                                k.op("pe", lambda e: e.matmul(po[m][:, :nq], lhsT=vb[hb][:, kc, :], rhs=ptl[i3][:, m * 512:m * 512 + nq],
                                                              start=(ci == 0), stop=(ci == nch - 1)), [Bvb[hb], Bptl[i3]], [Bpo[m]])
                            av = acc2[:].rearrange("p (m n) -> p m n", m=2)[:, :, :nq]
                            pv_ = ptl[i3][:].rearrange("p (m n) -> p m n", m=2)[:, :, :nq]
                            if ci == 0:
                                k.op("dve", lambda e: e.tensor_copy(out=av, in_=pv_), [Bptl[i3]], [Bacc2])
                            elif ci % 3 == 2:
                                for m in range(2):
                                    k.op("pe", lambda e: e.matmul(pf[m][:, :nq], lhsT=cB[:, C_ONE:C_ONE + 128], rhs=ptl[i3][:, m * 512:m * 512 + nq],
                                                                  start=(ci == 2), stop=False), [self.BcB, Bptl[i3]], [Bpf[m]])
                            else:
                                k.op("dve", lambda e: e.tensor_tensor(out=av, in0=av, in1=pv_, op=ALU.add), [Bptl[i3], Bacc2], [Bacc2])
                        for ci in range(nch + 1):
                            if ci < nch:
                                qk_pair(ci)
                            if ci - 1 >= 0:
                                pv_pair(ci - 1)
                        for m in range(2):
                            k.op("pe", lambda e: e.matmul(pf[m][:, :nq], lhsT=cF[:, C_ONE:C_ONE + 128], rhs=acc[m][:, :nq],
                                                          start=(nch < 3), stop=True), [self.BcF, Bacc[m]], [Bpf[m]])
                            k.op("act", lambda e: e.activation(out=r[m][:, :nq], in_=pf[m][:, :nq], func=AF.Ln), [Bpf[m]], [Br[m]])
                            k.op("act", lambda e: e.activation(out=r[m][:, :nq], in_=r[m][:, :nq], func=AF.Exp, scale=-1.0), [Br[m]], [Br[m]])
                            k.op("dve", lambda e: e.tensor_tensor(out=tt[m][:, :nq], in0=po[m][:, :nq], in1=r[m][:, :nq], op=ALU.mult),
                                 [Bpo[m], Br[m]], [Btt[m]])
                        k.op("dve", lambda e: e.scalar_tensor_tensor(out=od[:, :nq], in0=tt[1][:, :nq], scalar=lamv[:, 3:4], in1=tt[0][:, :nq],
                                                                     op0=ALU.mult, op1=ALU.add), [Btt[0], Btt[1], Blam], [Bod])
                        k.op("act", lambda e: e.activation(out=sqd[:, :nq], in_=od[:, :nq], func=AF.Square), [Bod], [Bsqd])
                        k.op("pe", lambda e: e.matmul(pf[0][:, :nq], lhsT=cB[:, C_ONE:C_ONE + 128], rhs=sqd[:, :nq], start=True, stop=True),
                             [self.BcB, Bsqd], [Bpf[0]])
                        k.op("act", lambda e: e.activation(out=r[0][:, :nq], in_=pf[0][:, :nq], func=AF.Ln, scale=1.0 / 128,
                                                           bias=self.epsT[:, 0:1]), [Bpf[0], self.Beps], [Br[0]])
                        k.op("act", lambda e: e.activation(out=r[0][:, :nq], in_=r[0][:, :nq], func=AF.Exp, scale=-0.5), [Br[0]], [Br[0]])
                        k.op("dve", lambda e: e.scalar_tensor_tensor(out=Oall[:, 4 + h, qc0:qc0 + nq], in0=od[:, :nq], scalar=sgs[:, 0:1],
                                                                     in1=r[0][:, :nq], op0=ALU.mult, op1=ALU.mult), [Bod, Br[0], Bsgs], [BO])
                k.barrier()
            self.outproj(Oall, BO, I["w0o"], 0, [(I["xin"].ap()[b * 128:(b + 1) * 128, :], S["hm"].ap()[b * 128:(b + 1) * 128, :], 0, b * 128)
                                                 for b in range(16)] +
                         [(I["xin"].ap()[T + b * 128:T + (b + 1) * 128, :], S["hm"].ap()[TO + b * 128:TO + (b + 1) * 128, :], 1, TO + b * 128)
                          for b in range(2)], "hm", exch=(S["hm"], S["row"], S["rowg"], S["row2"]), pre=wo_pre)

    def load_wo(self, st, w_dram, defer=False):
        k = self.k
        wo = self.sb(st, "op_wo", [128, 8, D], BF16)
        Bwo = Buf()

        def issue():
            for kk in range(8):
                k.dma("pool", wo[:, kk, :], w_dram.ap()[kk * 128:(kk + 1) * 128, :], writes=[Bwo])
        if defer:
            self._wo_issue = issue
        else:
            issue()
        return wo, Bwo

    def outproj(self, Oall, BO, w_dram, gi, blocks, outname, xrd=(), exch=None, pre=None):
        k = self.k
        with ExitStack() as st:
            if pre is None:
                wo, Bwo = self.load_wo(st, w_dram)
            else:
                wo, Bwo = pre
            py = [self.ps(st, "op_py%d" % i) for i in range(4)]
            Bpy = [Buf() for _ in range(4)]
            xt = [self.sb(st, "op_xt%d" % i, [128, D]) for i in range(2)]
            Bxt = [Buf(), Buf()]
            t = [self.sb(st, "op_t%d" % i, [128, D]) for i in range(2)]
            Bt = [Buf(), Buf()]
            Bdst = getattr(self, "Bd_" + outname, None)
            if Bdst is None:
                Bdst = Buf(outname)
                setattr(self, "Bd_" + outname, Bdst)
            if exch is not None:
                blocks = [blocks[15]] + blocks[:15] + blocks[16:]
            for bi, (src, dst, v, col0) in enumerate(blocks):
                j = bi % 2
                k.dma("sp", xt[j][:], src, reads=list(xrd), writes=[Bxt[j]])
                for half in range(2):
                    p = py[2 * j + half]
                    for kk in range(8):
                        k.op("pe", lambda e: e.matmul(p[:], lhsT=Oall[:, kk, col0:col0 + 128], rhs=wo[:, kk, half * 512:(half + 1) * 512],
                                                      start=(kk == 0), stop=(kk == 7)), [BO, Bwo], [Bpy[2 * j + half]])
                    k.op("dve", lambda e: e.tensor_tensor(out=t[j][:, half * 512:(half + 1) * 512], in0=p[:],
                                                          in1=self.gbc[:, v, gi, half * 512:(half + 1) * 512], op=ALU.mult),
                         [Bpy[2 * j + half], self.Bgbc], [Bt[j]])
                k.op("dve", lambda e: e.tensor_tensor(out=t[j][:], in0=t[j][:], in1=xt[j][:], op=ALU.add), [Bt[j], Bxt[j]], [Bt[j]])
                k.dma("sp", dst, t[j][:], reads=[Bt[j]], writes=[Bdst])
                if exch is not None and bi == 0:
                    src_t, row, rowg, row2 = exch
                    Brow, Browg = Buf(), Buf()
                    k.dma("pool", row.ap(), src_t.ap()[TO - 1:TO, :], reads=[Bdst], writes=[Brow])
                    k.allgather([[0, 1], [2, 3], [4, 5], [6, 7]], row.ap(), rowg.ap(), reads=[Brow], writes=[Browg])
            if exch is not None:
                self.exchange_b(rowg, row2, Browg)
            k.barrier()

    def exchange_b(self, rowg, row2, Browg):
        k = self.k
        Brow2 = Buf()
        with ExitStack() as st:
            rg = self.sb(st, "xr_rg", [1, 2 * D])
            hr = self.sb(st, "xr_hr", [1, D])
            Brg, Bhr = Buf(), Buf()
            k.dma("sp", rg[:], rowg.ap().rearrange("a d -> (a d)").rearrange("(o n) -> o n", o=1), reads=[Browg], writes=[Brg])
            k.op("dve", lambda e: e.tensor_scalar(out=hr[:], in0=rg[:, 0:D], scalar1=self.sel[0:1, 0:1], scalar2=None, op0=ALU.mult),
                 [Brg, self.Bsel], [Bhr])
            k.op("dve", lambda e: e.scalar_tensor_tensor(out=hr[:], in0=rg[:, D:2 * D], scalar=self.sel[0:1, 1:2], in1=hr[:],
                                                         op0=ALU.mult, op1=ALU.add), [Brg, Bhr, self.Bsel], [Bhr])
            k.dma("sp", row2.ap(), hr[:], reads=[Bhr], writes=[Brow2])
            k.barrier()
        self.Brow2 = Brow2
        self.row2ap = row2.ap()

    def ffn(self, l):
        k, I, S = self.k, self.I, self.S
        cB, cF = self.cB, self.cF
        src = S["hm"] if l == 0 else S["hm1"]
        Bsrc = getattr(self, "Bd_hm" if l == 0 else "Bd_hm1")
        row2 = S["row2"] if l == 0 else S["rowg2"]
        passes = [(0, 1024, None, src.ap()[1024:1025, :], 0), (1024, 1024, src.ap()[1023:1024, :], self.row2ap, 0)]
        if l == 0:
            passes.append((TO, 256, None, None, 1))
        Bh1 = Buf("h1")
        self.Bd_h1 = Bh1
        for (r0, W, left, right, v) in passes:
            with ExitStack() as so, ExitStack() as st:
                X = self.sb(so, "f_X", [128, 8, W + 2], BF16)
                BX = Buf()
                act = self.sb(so, "f_act", [128, NJ, W], BF16)
                Bact = Buf()
                cwt = self.sb(so, "f_cw", [128, 44, 3])
                cbt = self.sb(so, "f_cb", [128, 44])
                Bcw = Buf()
                wd = self.sb(so, "f_wd", [128, NJ, D], BF16)
                Bwd = [Buf() for _ in range(NJ)]
                k.dma("sp", cwt[:], I["cw"].ap()[l], writes=[Bcw])
                k.dma("sp", cbt[:], I["cbias"].ap()[l], writes=[Bcw])
                wu = [self.sb(so, "f_wu%d" % i, [128, 8, 2, 256], BF16) for i in range(3)]
                Bwu = [Buf() for _ in range(3)]

                def wload(g):
                    g3 = g % 3
                    for part, coff in ((0, g * 256), (1, DFF + g * 256)):
                        k.dma("pool", wu[g3][:, :, part, :],
                              I["wup"].ap()[l, :, coff:coff + 256].rearrange("(k p) n -> p k n", p=128), writes=[Bwu[g3]])
                    for j_ in (2 * g, 2 * g + 1):
                        k.dma("pool", wd[:, j_, :], I["wdn"].ap()[l, j_ * 128:(j_ + 1) * 128, :], writes=[Bwd[j_]])

                wload(0)
                wload(1)
                sr = ExitStack()
                R = self.norm_res(sr, nbuf=3, npt=3)
                R["ms_eng"] = "dve"
                self.norm_seq(R, [([(src.ap()[r0 + bi * 128:r0 + (bi + 1) * 128, :], 0)], v, 1, X, BX, bi * 128, [Bsrc])
                                  for bi in range(W // 128)])
                if left is None and right is None:
                    k.op("pool", lambda e: e.memset(X[:, :, W:W + 2], 0.0), [], [BX])
                else:
                    la = left if left is not None else right
                    self.norm_block(R, [(la, 0), (right, 1)], v, 1, X, BX, W, rd=[Bsrc, self.Brow2])
                    if left is None:
                        k.op("pool", lambda e: e.memset(X[:, :, W:W + 1], 0.0), [], [BX])
                k.barrier()
                sr.close()
                hup = [self.sb(st, "f_hup%d" % i, [128, 2, W + 2]) for i in range(3)]
                Bhup = [Buf() for _ in range(3)]
                tcv = [[self.sb(st, "f_tcv%d_%d" % (i, p_), [128, W]) for p_ in range(2)] for i in range(2)]
                Btcv = [[Buf(), Buf()] for _ in range(2)]
                pu = [self.ps(st, "f_pu%d" % i) for i in range(4)]
                Bpu = [Buf() for _ in range(4)]
                ph = [self.ps(st, "f_ph%d" % i, [128, 16]) for i in range(2)]
                Bph = [Buf(), Buf()]
                pieces = [(c0, min(512, W - c0)) for c0 in range(0, W, 512)]
                cnt = [0]

                def mm(j):
                    j3 = j % 3
                    g3 = (j // 2) % 3
                    wo_ = (j % 2) * 128
                    hb = hup[j3]
                    for part in range(2):
                        for (c0, n) in pieces:
                            i4 = cnt[0] % 4
                            cnt[0] += 1
                            for kk in range(8):
                                k.op("pe", lambda e: e.matmul(pu[i4][:, :n], lhsT=wu[g3][:, kk, part, wo_:wo_ + 128],
                                                              rhs=X[:, kk, c0:c0 + n], start=(kk == 0), stop=(kk == 7)),
                                     [Bwu[g3], BX], [Bpu[i4]])
                            k.op("act", lambda e: e.activation(out=hb[:, part, 1 + c0:1 + c0 + n], in_=pu[i4][:, :n], func=AF.Copy),
                                 [Bpu[i4]], [Bhup[j3]])
                        for kk in range(8):
                            k.op("pe", lambda e: e.matmul(ph[part][:, 0:2], lhsT=wu[g3][:, kk, part, wo_:wo_ + 128],
                                                          rhs=X[:, kk, W:W + 2], start=(kk == 0), stop=(kk == 7)), [Bwu[g3], BX], [Bph[part]])
                        k.op("act", lambda e: e.activation(out=hb[:, part, 0:1], in_=ph[part][:, 0:1], func=AF.Copy),
                             [Bph[part]], [Bhup[j3]])
                        k.op("act", lambda e: e.activation(out=hb[:, part, W + 1:W + 2], in_=ph[part][:, 1:2], func=AF.Copy),
                             [Bph[part]], [Bhup[j3]])

                def conv_a(j):
                    j3, j2 = j % 3, j % 2
                    hb = hup[j3]
                    for part in range(2):
                        tb, Btb = tcv[j2][part], Btcv[j2][part]
                        ci = part * NJ + j
                        k.op("act", lambda e: e.activation(out=tb[:], in_=hb[:, part, 0:W], func=AF.Copy, scale=cwt[:, ci, 0:1]),
                             [Bhup[j3], Bcw], [Btb])
                        k.op("dve", lambda e: e.scalar_tensor_tensor(out=tb[:], in0=hb[:, part, 1:W + 1], scalar=cwt[:, ci, 1:2], in1=tb[:],
                                                                     op0=ALU.mult, op1=ALU.add), [Bhup[j3], Bcw, Btb], [Btb])
                        k.op("dve", lambda e: e.scalar_tensor_tensor(out=tb[:], in0=hb[:, part, 2:W + 2], scalar=cwt[:, ci, 2:3], in1=tb[:],
                                                                     op0=ALU.mult, op1=ALU.add), [Bhup[j3], Bcw, Btb], [Btb])

                def conv_b(j):
                    j2 = j % 2
                    tu_, tg_ = tcv[j2]
                    Btu_, Btg_ = Btcv[j2]
                    k.op("act", lambda e: e.activation(out=tg_[:], in_=tg_[:], func=AF.Silu, bias=cbt[:, NJ + j:NJ + j + 1]),
                         [Btg_, Bcw], [Btg_])
                    k.op("dve", lambda e: e.scalar_tensor_tensor(out=act[:, j, :], in0=tu_[:], scalar=cbt[:, j:j + 1], in1=tg_[:],
                                                                 op0=ALU.add, op1=ALU.mult), [Btu_, Btg_, Bcw], [Bact])

                for j in range(NJ + 2):
                    if j % 2 == 0 and j // 2 + 2 < NJ // 2:
                        wload(j // 2 + 2)
                    if j < NJ:
                        mm(j)
                    if 0 <= j - 1 < NJ:
                        conv_a(j - 1)
                    if 0 <= j - 2 < NJ:
                        conv_b(j - 2)
                k.barrier()
                st.close()
                st = so
                py = [self.ps(st, "f_py%d" % i) for i in range(4)]
                Bpy = [Buf() for _ in range(4)]
                hmb = [self.sb(st, "f_hm%d" % i, [128, D]) for i in range(2)]
                Bhmb = [Buf(), Buf()]
                tt = [self.sb(st, "f_t%d" % i, [128, D]) for i in range(2)]
                Btt = [Buf(), Buf()]
                fng = self.sb(st, "f_fng", [128, D])
                Bfng = Buf()
                fss = [self.sb(st, "f_ss%d" % i, [128, 4]) for i in range(2)]
                Bfss = [Buf(), Buf()]
                fjunk = self.sb(st, "f_junk", [128, D], BF16)
                Bfjunk = Buf()
                if l == 1:
                    k.dma("sp", fng[:], dap(I["fng"], 0, [[0, 128], [1, D]]), writes=[Bfng])
                for bi in range(W // 128):
                    jj = bi % 2
                    rr = r0 + bi * 128
                    k.dma("sp", hmb[jj][:], src.ap()[rr:rr + 128, :], reads=[Bsrc], writes=[Bhmb[jj]])
                    for half in range(2):
                        pi = 2 * jj + half
                        for j in range(NJ):
                            k.op("pe", lambda e: e.matmul(py[pi][:], lhsT=act[:, j, bi * 128:(bi + 1) * 128],
                                                          rhs=wd[:, j, half * 512:(half + 1) * 512], start=(j == 0), stop=(j == NJ - 1)),
                                 [Bact, Bwd[j]], [Bpy[pi]])
                        k.op("dve", lambda e: e.tensor_tensor(out=tt[jj][:, half * 512:(half + 1) * 512], in0=py[pi][:],
                                                              in1=self.gbc[:, v, 1, half * 512:(half + 1) * 512], op=ALU.mult),
                             [Bpy[pi], self.Bgbc], [Btt[jj]])
                    k.op("pool", lambda e: e.tensor_tensor(out=tt[jj][:], in0=tt[jj][:], in1=hmb[jj][:], op=ALU.add),
                         [Btt[jj], Bhmb[jj]], [Btt[jj]])
                    if l == 0:
                        k.dma("sp", S["h1"].ap()[rr:rr + 128, :], tt[jj][:], reads=[Btt[jj]], writes=[Bh1])
                    else:
                        ss, Bss = fss[jj], Bfss[jj]
                        k.op("dve", lambda e: e.memset(ss[:], 0.0), [], [Bss])
                        k.op("act", lambda e: e.activation(out=fjunk[:], in_=tt[jj][:], func=AF.Square, accum_out=ss[:, 0:1]),
                             [Btt[jj], Bss], [Bfjunk, Bss])
                        k.op("act", lambda e: e.activation(out=ss[:, 1:2], in_=ss[:, 0:1], func=AF.Ln, scale=1.0 / D,
                                                           bias=self.epsT[:, 0:1]), [Bss, self.Beps], [Bss])
                        k.op("act", lambda e: e.activation(out=ss[:, 2:3], in_=ss[:, 1:2], func=AF.Exp, scale=-0.5), [Bss], [Bss])
                        k.op("dve", lambda e: e.scalar_tensor_tensor(out=hmb[jj][:], in0=tt[jj][:], scalar=ss[:, 2:3], in1=fng[:],
                                                                     op0=ALU.mult, op1=ALU.mult), [Btt[jj], Bss, Bfng], [Bhmb[jj]])
                        k.dma("sp", self.y.ap()[rr:rr + 128, :], hmb[jj][:], reads=[Bhmb[jj]], writes=[self.By])
                k.barrier()

    def layer1(self):
        k, I, S = self.k, self.I, self.S
        sw = ExitStack()
        w1 = self.sb(sw, "g_w1", [128, 8, 3072], BF16)
        Bw1 = Buf()
        for g in range(6):
            k.dma("pool", w1[:, :, g * 512:(g + 1) * 512], I["w1"].ap()[:, g * 512:(g + 1) * 512].rearrange("(k p) n -> p k n", p=128),
                  writes=[Bw1])
        self.compute_mod(1, False)
        self.gla(w1, Bw1)
        sw.close()
        with ExitStack() as st:
            Oall = self.sb(st, "Oall1", [128, 8, TO], BF16)
            BO = Buf()
            k.dma("sp", Oall[:], S["oall1"].ap().rearrange("c p n -> p c n"), reads=[self.Boall1], writes=[BO])
            self.outproj(Oall, BO, I["w1o"], 0, [(S["h1"].ap()[b * 128:(b + 1) * 128, :], S["hm1"].ap()[b * 128:(b + 1) * 128, :], 0, b * 128)
                                                 for b in range(16)], "hm1", xrd=[self.Bd_h1], exch=(S["hm1"], S["row"], S["rowg"], S["row2"]))
        if self.debug == "e":
            return
        self.ffn(1)

    def gla(self, w1, Bw1):
        k, I, S = self.k, self.I, self.S
        cB, cF = self.cB, self.cF
        Bh1 = self.Bd_h1
        self.Boall1 = Buf("oall1")
        Bofwd = Buf("ofwd")
        with ExitStack() as st:
            R = self.norm_res(st)
            gw1 = self.sb(st, "g_gw1", [128, 8, 32], BF16)
            gw2 = self.sb(st, "g_gw2", [32, 2, 512], BF16)
            glng = self.sb(st, "g_glng", [128, 2])
            Bgw = Buf()
            with ExitStack() as s2:
                gw1f = self.sb(s2, "g_gw1f", [128, 8, 32])
                gw2f = self.sb(s2, "g_gw2f", [32, 2, 512])
                Bgwf = Buf()
                k.dma("sp", gw1f[:], I["gw1"].ap(), writes=[Bgwf])
                k.dma("sp", gw2f[:], I["gw2"].ap(), writes=[Bgwf])
                k.dma("sp", glng[:], I["glng"].ap(), writes=[Bgw])
                k.op("dve", lambda e: e.tensor_copy(out=gw1[:], in_=gw1f[:]), [Bgwf], [Bgw])
                k.op("dve", lambda e: e.tensor_copy(out=gw2[:], in_=gw2f[:]), [Bgwf], [Bgw])
                k.barrier()
            X = self.sb(st, "g_X", [128, 8, 512], BF16)
            BX = Buf()
            qT = self.sb(st, "g_qT", [128, 4, 512])
            kT = self.sb(st, "g_kT", [128, 4, 512])
            ktm = self.sb(st, "g_ktm", [128, 4, 512])
            vtm = self.sb(st, "g_vtm", [128, 4, 1024], BF16)
            latm = self.sb(st, "g_latm", [128, 4, 512], BF16)
            sgT = self.sb(st, "g_sgT", [128, 8, 512], BF16)
            BqT, BkT, Bktm, Bvtm, Blatm, BsgT = Buf(), Buf(), Buf(), Buf(), Buf(), Buf()
            raug = self.sb(st, "g_raug", [32, 512], BF16)
            Braug = Buf()
            k.op("pool", lambda e: e.memset(raug[:], 1.0), [], [Braug])
            ez = self.sb(st, "g_ez", [128, 512])
            Bez = Buf()
            ot = self.sb(st, "g_ot", [128, 8, 512])
            Bot = Buf()
            oo = self.sb(st, "g_oo", [128, 8, 512], BF16)
            Boo = Buf()
            sqt = self.sb(st, "g_sqt", [128, 2, 512], BF16)
            Bsqt = Buf()
            rs = self.sb(st, "g_rs", [128, 512])
            Brs = Buf()
            tf = self.sb(st, "g_tf", [128, 512])
            Btf = Buf()
            SfA = self.sb(st, "g_SfA", [128, 4, 256])
            Sf = [SfA[:, h, :] for h in range(4)]
            Sb = [self.sb(st, "g_Sb%d" % h, [128, 256], BF16) for h in range(4)]
            BSf = [Buf() for _ in range(4)]
            BSb = [Buf() for _ in range(4)]
            epm2 = [[self.sb(st, "g_epm%d_%d" % (p_, h), [128, 2, 128]) for h in range(4)] for p_ in range(2)]
            qk2 = [[self.sb(st, "g_qk%d_%d" % (p_, h), [128, 2, 128], BF16) for h in range(4)] for p_ in range(2)]
            sTt2 = [[self.sb(st, "g_sT%d_%d" % (p_, h), [128, 128], BF16) for h in range(4)] for p_ in range(2)]
            ek2 = [[self.sb(st, "g_ek%d_%d" % (p_, h), [128, 128]) for h in range(4)] for p_ in range(2)]
            kh2 = [[self.sb(st, "g_kh%d_%d" % (p_, h), [128, 128], BF16) for h in range(4)] for p_ in range(2)]
            Bepm2, Bqk2, BsTt2, Bek2, Bkh2 = ([[Buf() for _ in range(4)] for _ in range(2)] for _ in range(5))
            par = [0]
            G = [self.ps(st, "g_G%d" % i) for i in range(6)]
            BG = [Buf() for _ in range(6)]
            for h in range(4):
                k.op("pool", lambda e: e.memset(Sf[h][:], 0.0), [], [BSf[h]])
                k.op("pool", lambda e: e.memset(Sb[h][:], 0.0), [], [BSb[h]])
            gc = [0]

            def nextG():
                i = gc[0] % 6
                gc[0] += 1
                return G[i], BG[i]

            def prep(cb, d, full):
                tri = C_TRIF if d == 0 else C_TRIR
                suf = C_SUFF if d == 0 else C_PRER
                last = 127 if d == 0 else 0
                c0 = cb * 128
                pp_ = par[0] % 2
                par[0] += 1
                epm, qk, sTt, ek, kh = epm2[pp_], qk2[pp_], sTt2[pp_], ek2[pp_], kh2[pp_]
                Bepm, Bqk, BsTt, Bek, Bkh = Bepm2[pp_], Bqk2[pp_], BsTt2[pp_], Bek2[pp_], Bkh2[pp_]
                HS = [slice(h * 128, (h + 1) * 128) for h in range(4)]
                pBs = []
                for h in range(4):
                    pB, BpB = nextG()
                    pBs.append((pB, BpB))
                    k.op("pe", lambda e: e.matmul(pB[:, :128], lhsT=latm[:, cb, HS[h]], rhs=cB[:, tri:tri + 128], start=True, stop=True),
                         [Blatm, self.BcB], [BpB])
                for h in range(4):
                    pB, BpB = pBs[h]
                    k.op("act", lambda e: e.activation(out=epm[h][:, 0, :], in_=pB[:, :128], func=AF.Exp), [BpB], [Bepm[h]])
                    if full:
                        k.op("act", lambda e: e.activation(out=epm[h][:, 1, :], in_=pB[:, :128], func=AF.Exp, scale=-1.0), [BpB], [Bepm[h]])
                pXs = []
                for h in range(4):
                    pX, BpX = nextG()
                    pXs.append((pX, BpX))
                    k.op("pe", lambda e: e.matmul(pX[:, :128], lhsT=cB[:, suf:suf + 128], rhs=latm[:, cb, HS[h]], start=True, stop=True),
                         [Blatm, self.BcB], [BpX])
                for h in range(4):
                    pX, BpX = pXs[h]
                    k.op("act", lambda e: e.activation(out=ek[h][:], in_=pX[:, :128], func=AF.Exp), [BpX], [Bek[h]])
                if full:
                    for h in range(4):
                        k.op("dve", lambda e: e.tensor_tensor(out=qk[h][:, 0, :], in0=qT[:, h, c0:c0 + 128], in1=epm[h][:, 0, :], op=ALU.mult),
                             [BqT, Bepm[h]], [Bqk[h]])
                        k.op("dve", lambda e: e.tensor_tensor(out=qk[h][:, 1, :], in0=kT[:, h, c0:c0 + 128], in1=epm[h][:, 1, :], op=ALU.mult),
                             [BkT, Bepm[h]], [Bqk[h]])
                    pSs = []
                    for h in range(4):
                        pS, BpS = nextG()
                        pSs.append((pS, BpS))
                        k.op("pe", lambda e: e.matmul(pS[:, :128], lhsT=qk[h][:, 1, :], rhs=qk[h][:, 0, :], start=True, stop=True),
                             [Bqk[h]], [BpS])
                for h in range(4):
                    k.op("dve", lambda e: e.tensor_tensor(out=kh[h][:], in0=ktm[:, cb, HS[h]], in1=ek[h][:], op=ALU.mult),
                         [Bktm, Bek[h]], [Bkh[h]])
                if full:
                    for h in range(4):
                        pS, BpS = pSs[h]
                        k.op("dve", lambda e: e.tensor_tensor(out=sTt[h][:], in0=pS[:, :128], in1=cB[:, tri:tri + 128], op=ALU.mult),
                             [BpS, self.BcB], [BsTt[h]])
                return dict(cb=cb, d=d, full=full, pp_=pp_, last=last, c0=c0)

            def serial(cx):
                cb, d, full, pp_, last, c0 = cx["cb"], cx["d"], cx["full"], cx["pp_"], cx["last"], cx["c0"]
                epm, qk, sTt, ek, kh = epm2[pp_], qk2[pp_], sTt2[pp_], ek2[pp_], kh2[pp_]
                Bepm, Bqk, BsTt, Bek, Bkh = Bepm2[pp_], Bqk2[pp_], BsTt2[pp_], Bek2[pp_], Bkh2[pp_]
                if full:
                    for h in range(4):
                        for e2 in range(2):
                            pO, BpO = nextG()
                            vs_ = slice(h * 256 + e2 * 128, h * 256 + (e2 + 1) * 128)
                            k.op("pe", lambda e: e.matmul(pO[:, :128], lhsT=vtm[:, cb, vs_], rhs=sTt[h][:], start=True, stop=False),
                                 [Bvtm, BsTt[h]], [BpO])
                            k.op("pe", lambda e: e.matmul(pO[:, :128], lhsT=Sb[h][:, e2 * 128:(e2 + 1) * 128], rhs=qk[h][:, 0, :],
                                                          start=False, stop=True), [BSb[h], Bqk[h]], [BpO])
                            if d == 0:
                                k.op("act", lambda e: e.activation(out=ot[:, 2 * h + e2, c0:c0 + 128], in_=pO[:, :128], func=AF.Copy),
                                     [BpO], [Bot])
                            else:
                                k.op("dve", lambda e: e.tensor_tensor(out=ot[:, 2 * h + e2, c0:c0 + 128], in0=ot[:, 2 * h + e2, c0:c0 + 128],
                                                                      in1=pO[:, :128], op=ALU.add), [BpO, Bot], [Bot])
                for h in range(4):
                    pD, BpD = nextG()
                    k.op("pe", lambda e: e.matmul(pD[:, :256], lhsT=kh[h][:], rhs=vtm[:, cb, h * 256:(h + 1) * 256], start=True, stop=True),
                         [Bkh[h], Bvtm], [BpD])
                    k.op("dve", lambda e: e.scalar_tensor_tensor(out=Sf[h][:], in0=Sf[h][:], scalar=epm[h][:, 0, last:last + 1], in1=pD[:, :256],
                                                                 op0=ALU.mult, op1=ALU.add), [BSf[h], Bepm[h], BpD], [BSf[h]])
                    k.op("act", lambda e: e.activation(out=Sb[h][:], in_=Sf[h][:], func=AF.Copy), [BSf[h]], [BSb[h]])

            def xblock(r0, v, b):
                self.norm_block(R, [(S["h1"].ap()[r0 + b * 128:r0 + (b + 1) * 128, :], 0)], v, 0, X, BX, b * 128, rd=[Bh1])

            def xblock_a(r0, v, b):
                return self.norm_a(R, [(S["h1"].ap()[r0 + b * 128:r0 + (b + 1) * 128, :], 0)], v, 0, X, BX, b * 128, rd=[Bh1])

            def tile(r0, v, d, full, nblk, build_x=True, nxt=None):
                n = nblk * 128
                if build_x:
                    for b in range(nblk):
                        xblock(r0, v, b)
                if full:
                    for h in range(4):
                        for (dst, Bdst, coff, sc) in ((qT, BqT, h * 128, 128 ** -0.5), (kT, BkT, 512 + h * 128, 1.0)):
                            p, Bp = nextG()
                            for kk in range(8):
                                k.op("pe", lambda e: e.matmul(p[:, :n], lhsT=w1[:, kk, coff:coff + 128], rhs=X[:, kk, :n],
                                                              start=(kk == 0), stop=(kk == 7)), [Bw1, BX], [Bp])
                            k.op("act", lambda e: e.activation(out=dst[:, h, :n], in_=p[:, :n], func=AF.Copy, scale=sc), [Bp], [Bdst])
                    if d == 1:
                        for c in range(8):
                            p, Bp = nextG()
                            for kk in range(8):
                                k.op("pe", lambda e: e.matmul(p[:, :n], lhsT=w1[:, kk, 2048 + c * 128:2048 + (c + 1) * 128], rhs=X[:, kk, :n],
                                                              start=(kk == 0), stop=(kk == 7)), [Bw1, BX], [Bp])
                            k.op("act", lambda e: e.activation(out=sgT[:, c, :n], in_=p[:, :n], func=AF.Silu), [Bp], [BsgT])
                for b in range(nblk):
                    bs = slice(b * 128, (b + 1) * 128)
                    p, Bp = nextG()
                    for kk in range(8):
                        k.op("pe", lambda e: e.matmul(p[:, :], lhsT=X[:, kk, bs], rhs=w1[:, kk, 512:1024], start=(kk == 0), stop=(kk == 7)),
                             [Bw1, BX], [Bp])
                    k.op("act", lambda e: e.activation(out=ktm[:, b, :], in_=p[:, :], func=AF.Copy), [Bp], [Bktm])
                    for i2 in range(2):
                        p, Bp = nextG()
                        for kk in range(8):
                            k.op("pe", lambda e: e.matmul(p[:, :], lhsT=X[:, kk, bs], rhs=w1[:, kk, 1024 + i2 * 512:1536 + i2 * 512],
                                                          start=(kk == 0), stop=(kk == 7)), [Bw1, BX], [Bp])
                        k.op("act", lambda e: e.activation(out=vtm[:, b, i2 * 512:(i2 + 1) * 512], in_=p[:, :], func=AF.Copy), [Bp], [Bvtm])
                p, Bp = nextG()
                for kk in range(8):
                    k.op("pe", lambda e: e.matmul(p[0:16, :n], lhsT=gw1[:, kk, d * 16:(d + 1) * 16], rhs=X[:, kk, :n],
                                                  start=(kk == 0), stop=(kk == 7)), [Bgw, BX], [Bp])
                k.op("act", lambda e: e.activation(out=raug[0:16, :n], in_=p[0:16, :n], func=AF.Copy), [Bp], [Braug])
                for b in range(nblk):
                    p, Bp = nextG()
                    k.op("pe", lambda e: e.matmul(p[:, :], lhsT=raug[:, b * 128:(b + 1) * 128], rhs=gw2[:, d, :], start=True, stop=True),
                         [Braug, Bgw], [Bp])
                    k.op("act", lambda e: e.activation(out=ez[:], in_=p[:, :], func=AF.Exp, scale=-1.0), [Bp], [Bez])
                    k.op("act", lambda e: e.activation(out=ez[:], in_=ez[:], func=AF.Ln, bias=cF[:, C_ONE:C_ONE + 1]), [Bez, self.BcF], [Bez])
                    k.op("dve", lambda e: e.tensor_scalar(out=latm[:, b, :], in0=ez[:], scalar1=-1.0 / 16.0, scalar2=None, op0=ALU.mult),
                         [Bez], [Blatm])
                order = list(range(nblk) if d == 0 else range(nblk - 1, -1, -1))
                nxt_blocks = [] if nxt is None else [(nxt[0], nxt[1], b) for b in range(nxt[2])]
                cx = prep(order[0], d, full)
                pend = None
                for i in range(len(order)):
                    cxn = prep(order[i + 1], d, full) if i + 1 < len(order) else None
                    if pend is not None:
                        self.norm_b(R, pend)
                        pend = None
                    if nxt_blocks:
                        pend = xblock_a(*nxt_blocks.pop(0))
                    serial(cx)
                    cx = cxn
                while nxt_blocks or pend is not None:
                    if pend is not None:
                        self.norm_b(R, pend)
                        pend = None
                    if nxt_blocks:
                        pend = xblock_a(*nxt_blocks.pop(0))

            tile(TO, 1, 0, False, 2, build_x=True, nxt=(0, 0, 4))
            for ti in range(4):
                nx = ((ti + 1) * 512, 0, 4) if ti < 3 else (3 * 512, 0, 4)
                tile(ti * 512, 0, 0, True, 4, build_x=False, nxt=nx)
                k.dma("sp", S["ofwd"].ap()[:, :, ti * 512:(ti + 1) * 512].rearrange("c p n -> p c n"), ot[:], reads=[Bot], writes=[Bofwd])
            Bst, Bstg = Buf(), Buf()
            k.dma("sp", S["st"].ap(), SfA[:].rearrange("p h n -> p (h n)"), reads=[BSf[0], BSf[1], BSf[2], BSf[3]], writes=[Bst])
            k.allgather([[0, 1], [2, 3], [4, 5], [6, 7]], S["st"].ap(), S["stg"].ap(), reads=[Bst], writes=[Bstg])
            sg2 = ot[:].rearrange("p c n -> p (c n)")[:, 0:2048].rearrange("p (r n) -> p r n", r=2)
            k.dma("sp", sg2, S["stg"].ap().rearrange("(r p) n -> p r n", p=128), reads=[Bstg, Bofwd], writes=[Bot])
            for h in range(4):
                k.op("dve", lambda e: e.tensor_scalar(out=Sf[h][:], in0=sg2[:, 0, h * 256:(h + 1) * 256], scalar1=self.sel[:, 0:1],
                                                      scalar2=None, op0=ALU.mult), [Bot, self.Bsel], [BSf[h]])
                k.op("dve", lambda e: e.scalar_tensor_tensor(out=Sf[h][:], in0=sg2[:, 1, h * 256:(h + 1) * 256], scalar=self.sel[:, 1:2],
                                                             in1=Sf[h][:], op0=ALU.mult, op1=ALU.add), [Bot, self.Bsel, BSf[h]], [BSf[h]])
                k.op("act", lambda e: e.activation(out=Sb[h][:], in_=Sf[h][:], func=AF.Copy), [BSf[h]], [BSb[h]])
            for ti in range(3, -1, -1):
                k.dma("sp", ot[:], S["ofwd"].ap()[:, :, ti * 512:(ti + 1) * 512].rearrange("c p n -> p c n"), reads=[Bofwd, BSf[0], BSf[1], BSf[2], BSf[3]],
                      writes=[Bot])
                tile(ti * 512, 0, 1, True, 4, build_x=False, nxt=(((ti - 1) * 512, 0, 4) if ti > 0 else None))
                for h in range(4):
                    k.op("act", lambda e: e.activation(out=sqt[:], in_=ot[:, 2 * h:2 * h + 2, :], func=AF.Square), [Bot], [Bsqt])
                    pN, BpN = nextG()
                    for e2 in range(2):
                        k.op("pe", lambda e: e.matmul(pN[:, :], lhsT=cB[:, C_ONE:C_ONE + 128], rhs=sqt[:, e2, :], start=(e2 == 0), stop=(e2 == 1)),
                             [self.BcB, Bsqt], [BpN])
                    k.op("act", lambda e: e.activation(out=rs[:], in_=pN[:, :], func=AF.Ln, scale=1.0 / 256, bias=self.epsT[:, 0:1]),
                         [BpN, self.Beps], [Brs])
                    k.op("act", lambda e: e.activation(out=rs[:], in_=rs[:], func=AF.Exp, scale=-0.5), [Brs], [Brs])
                    for e2 in range(2):
                        c = 2 * h + e2
                        k.op("dve", lambda e: e.scalar_tensor_tensor(out=tf[:], in0=ot[:, c, :], scalar=glng[:, e2:e2 + 1], in1=rs[:],
                                                                     op0=ALU.mult, op1=ALU.mult), [Bot, Bgw, Brs], [Btf])
                        k.op("dve", lambda e: e.tensor_tensor(out=oo[:, c, :], in0=tf[:], in1=sgT[:, c, :], op=ALU.mult), [Btf, BsgT], [Boo])
                k.dma("sp", S["oall1"].ap()[:, :, ti * 512:(ti + 1) * 512].rearrange("c p n -> p c n"), oo[:], reads=[Boo], writes=[self.Boall1])
            k.barrier()


def _consts():
    c = np.zeros((128, NCOL), np.float32)
    r = np.arange(128)
    c[r, C_ID + r] = 1.0
    sw = np.where(r % 64 < 32, r + 32, r - 32)
    c[sw, C_PERM + r] = 1.0
    c[:, C_BD:C_BD + 128] = (r[:, None] // 64 == r[None, :] // 64)
    c[0, C_ELO:C_ELO + 128] = 1.0
    c[64, C_EHI:C_EHI + 128] = 1.0
    c[:, C_OLO:C_OLO + 64] = 1.0
    c[:, C_OHI + 64:C_OHI + 128] = 1.0
    c[:, C_ONE:C_ONE + 128] = 1.0
    u = np.arange(640)
    c[:, C_BAND:C_BAND + 640] = ((u[None, :] >= r[:, None] + 128) & (u[None, :] <= r[:, None] + 384))
    s_, t_ = r[:, None], r[None, :]
    c[:, C_TRIF:C_TRIF + 128] = (s_ <= t_)
    c[:, C_SUFF:C_SUFF + 128] = (s_ > t_)
    c[:, C_TRIR:C_TRIR + 128] = (s_ >= t_)
    c[:, C_PRER:C_PRER + 128] = (s_ < t_)
    return c


def _rope_tables():
    rows = T // 64
    row = np.repeat(np.arange(rows, dtype=np.float32), 64)
    col = np.tile(np.arange(64, dtype=np.float32), rows)
    inv = (10000.0 ** (-np.arange(0, 32, 2, dtype=np.float32) / 32)).astype(np.float32)
    ang = np.concatenate([row[:, None] * inv, col[:, None] * inv], axis=-1).astype(np.float32)
    return np.cos(ang).astype(np.float32), np.sin(ang).astype(np.float32)


def _perm64():
    return np.concatenate([np.arange(0, 64, 2), np.arange(1, 64, 2)])


def _prep(inp):
    f = lambda a: np.ascontiguousarray(np.asarray(a, dtype=np.float32))
    x, c, ctx, c_ctx = f(inp["x"]), f(inp["c"]), f(inp["ctx"]), f(inp["c_ctx"])
    mod_w, mod_b = f(inp["mod_w"]), f(inp["mod_b"])
    p64 = _perm64()
    w_in = f(inp["attn_w_in"])[0]
    aq, ak, av, bq, bk, bv = np.split(w_in, [512, 640, 768, 1280, 1792], axis=1)
    cols = []
    for cchunk in range(4):
        for j in range(2):
            h = 2 * cchunk + j
            cols.append(aq[:, h * 64:(h + 1) * 64][:, p64])
    for kv in range(2):
        for j in range(2):
            cols.append(ak[:, kv * 64:(kv + 1) * 64][:, p64])
    for h in range(4):
        for m in range(2):
            cols.append(bq[:, (2 * h + m) * 64:(2 * h + m + 1) * 64][:, p64])
    for h in range(4):
        for m in range(2):
            cols.append(bk[:, (2 * h + m) * 64:(2 * h + m + 1) * 64][:, p64])
    cols.append(bv)
    cols.append(av)
    w0 = np.ascontiguousarray(np.concatenate(cols, axis=1))
    assert w0.shape == (D, 2432)
    cosT, sinT = _rope_tables()
    consts = _consts()
    fm = lambda vec, nk: np.ascontiguousarray(vec.reshape(nk, 128).T)
    shared = {
        "mod_w": mod_w, "mod_b": mod_b,
        "mod_bfm": np.stack([fm(mod_b[l], 48) for l in range(2)]),
        "n1g": np.stack([fm(f(inp["norm1_g"])[l], 8) for l in range(2)]),
        "n2g": np.stack([fm(f(inp["norm2_g"])[l], 8) for l in range(2)]),
        "fng": f(inp["final_norm_g"]),
        "w0": w0, "w0o": f(inp["attn_w_out"])[0],
        "sink": f(inp["attn_sink"])[0], "dlam": f(inp["diff_lambda"])[0].reshape(256),
        "subg": f(inp["diff_subln_g"])[0].reshape(128, 1),
        "consts": consts,
        "w1": f(inp["gla_w_in"])[0],
        "glng": fm(f(inp["gla_norm_g"])[0], 2),
        "w1o": f(inp["gla_w_out"])[0],
        "wup": f(inp["ffn_w_up"]), "wdn": f(inp["ffn_w_down"]),
        "cbias": np.stack([fm(f(inp["ffn_conv_b"])[l], 44) for l in range(2)]),
    }
    gw1 = f(inp["gla_gate_w1"])[0]
    gw2 = f(inp["gla_gate_w2"])[0]
    gb = f(inp["gla_gate_b"])[0]
    cwt = f(inp["ffn_conv_w"])
    maps = []
    for core in range(8):
        b, s = core // 2, core % 2
        xl = x[b]
        cl = ctx[b]
        tok = np.arange(T)
        dirs = [0, 1]
        cw = cwt
        if s == 1:
            xl = xl[::-1]
            cl = cl[::-1]
            tok = tok[::-1]
            dirs = [1, 0]
            cw = cwt[:, ::-1, :]
        m = dict(shared)
        m["xin"] = np.ascontiguousarray(np.concatenate([xl, cl], axis=0))
        cv = np.stack([fm(c[b], 8), fm(c_ctx, 8)], axis=-1)
        m["cvec"] = np.ascontiguousarray(cv)
        ct = np.ones((128, NTOK), np.float32)
        stb = np.zeros((128, NTOK), np.float32)
        rr = np.arange(128)
        ct[:, :T] = cosT[tok][:, rr % 32].T
        sgn = np.where(rr % 64 < 32, -1.0, 1.0).astype(np.float32)
        stb[:, :T] = sinT[tok][:, rr % 32].T * sgn[:, None]
        m["ctab"], m["stab"] = ct, stb
        sel = np.zeros((128, 2), np.float32)
        sel[:, 1 - s] = 1.0
        m["sel"] = sel
        g1 = np.zeros((128, 8, 32), np.float32)
        g2 = np.zeros((32, 2, 512), np.float32)
        for ld, d in enumerate(dirs):
            g1[:, :, ld * 16:(ld + 1) * 16] = gw1[d].reshape(8, 128, 16).transpose(1, 0, 2)
            g2[0:16, ld, :] = gw2[d]
            g2[16, ld, :] = gb[d]
        m["gw1"], m["gw2"] = g1, g2
        m["cw"] = np.ascontiguousarray(
            np.stack([cw[l].T.reshape(44, 128, 3).transpose(1, 0, 2) for l in range(2)]))
        maps.append(m)
    return maps


_CACHE = {}


def _run(inputs, debug=None):
    key = debug
    if key not in _CACHE:
        p = Prog(debug)
        nc = p.build()
        _CACHE[key] = (p, nc)
    p, nc = _CACHE[key]
    maps = _prep(inputs)
    res = run_bass_kernel_spmd(nc, maps, core_ids=list(range(8)))
    return p, res


def kernel(**inputs):
    p, res = _run(inputs, None)
    out = np.zeros((4, T, D), np.float32)
    for core in range(8):
        b, s = core // 2, core % 2
        y = np.asarray(res.results[core]["y"], dtype=np.float32)
        if s == 0:
            out[b, :TO] = y
        else:
            out[b, TO:] = y[::-1]
    return out
```

```python
import numpy as np
from contextlib import ExitStack
import concourse.bass as bass
import concourse.mybir as mybir
from concourse.bass_utils import run_bass_kernel_spmd

F32 = mybir.dt.float32
BF16 = mybir.dt.bfloat16
ALU = mybir.AluOpType
AF = mybir.ActivationFunctionType
AX = mybir.AxisListType

D = 1024
T = 4096
TO = 2048
L = 256
NTOK = T + L
DFF = 2816
NJ = 22
EPS = 1e-6
LAM0 = 0.8 - 0.6 * 1.0
SCALE = 0.125

C_ID, C_PERM, C_BD, C_ELO, C_EHI, C_OLO, C_OHI, C_ONE, C_BAND = 0, 128, 256, 384, 512, 640, 768, 896, 1024
C_TRIF, C_SUFF, C_TRIR, C_PRER = 1664, 1792, 1920, 2048
NCOL = 2176

DEBUG = None


class Buf:
    __slots__ = ("name", "w", "r")

    def __init__(self, name=""):
        self.name = name
        self.w = None
        self.r = {}


class Eng:
    def __init__(self, name, handle, sem):
        self.name = name
        self.h = handle
        self.sem = sem
        self.cnt = 0
        self.seen = {}


class K:
    def __init__(self, nc, stack, ndma=24):
        self.nc = nc
        self.sems = {}
        self.engs = {}
        for nm, h in (("pe", nc.tensor), ("act", nc.scalar), ("dve", nc.vector),
                      ("pool", nc.gpsimd), ("sp", nc.sync)):
            s = stack.enter_context(nc.semaphore("s_" + nm))
            self.sems[nm] = s
            self.engs[nm] = Eng(nm, h, s)
        self.dsems = []
        for i in range(ndma):
            s = stack.enter_context(nc.semaphore("d%d" % i))
            self.sems["d%d" % i] = s
            self.dsems.append(["d%d" % i, 0])
        self.dnext = 0
        self.sems["cc"] = stack.enter_context(nc.semaphore("ccs"))
        self.cccnt = 0
        self.ninstr = 0

    def _wait(self, e, key, val):
        if e.seen.get(key, 0) >= val:
            return
        e.h.wait_ge(self.sems[key], val)
        e.seen[key] = val
        self.ninstr += 1

    def _deps(self, e, reads, writes):
        for b in reads:
            if b.w is not None:
                k, v, en = b.w
                if not (en == "pe" and e.name == "pe"):
                    self._wait(e, k, v)
        for b in writes:
            if b.w is not None:
                k, v, en = b.w
                if en != e.name:
                    self._wait(e, k, v)
            for k, (v, en) in b.r.items():
                if en != e.name:
                    self._wait(e, k, v)

    def _mark(self, key, val, en, reads, writes):
        for b in reads:
            b.r[key] = (val, en)
        for b in writes:
            b.w = (key, val, en)
            b.r = {}

    def op(self, en, fn, reads=(), writes=()):
        e = self.engs[en]
        self._deps(e, reads, writes)
        ins = fn(e.h)
        e.cnt += 1
        ins.then_inc(e.sem, 1)
        self._mark(en, e.cnt, en, reads, writes)
        self.ninstr += 1
        return ins

    def dma(self, qn, out, in_, reads=(), writes=(), **kw):
        e = self.engs[qn]
        self._deps(e, reads, writes)
        slot = self.dsems[self.dnext]
        self.dnext = (self.dnext + 1) % len(self.dsems)
        key = slot[0]
        if slot[1] > 0:
            self._wait(e, key, slot[1])
        slot[1] += 16
        e.h.dma_start(out=out, in_=in_, **kw).then_inc(self.sems[key], 16)
        self._mark(key, slot[1], "dma", reads, writes)
        self.ninstr += 1

    def allgather(self, groups, in_ap, out_ap, reads=(), writes=()):
        e = self.engs["pool"]
        self._deps(e, reads, writes)
        self.cccnt += 1
        e.h.collective_compute("AllGather", ALU.bypass, replica_groups=groups,
                               ins=[in_ap], outs=[out_ap]).then_inc(self.sems["cc"], 1)
        self._mark("cc", self.cccnt, "cc", reads, writes)
        self.ninstr += 1

    def barrier(self):
        for e in self.engs.values():
            for o in self.engs.values():
                if o is not e and o.cnt > 0:
                    self._wait(e, o.name, o.cnt)
            for key, c in self.dsems:
                if c > 0:
                    self._wait(e, key, c)
            if self.cccnt > 0:
                self._wait(e, "cc", self.cccnt)


def dap(t, off, dims):
    return bass.AP(t, off, [list(d) for d in dims])


class Prog:
    def __init__(self, debug=None):
        self.debug = debug
        self.nc = bass.Bass("TRN2", target_bir_lowering=False)
        self.outs = []

    def din(self, name, shape, dt=F32):
        return self.nc.dram_tensor(name, list(shape), dt, kind="ExternalInput")

    def dscr(self, name, shape, dt=F32, dbg=False):
        kind = "ExternalOutput" if dbg else "Internal"
        t = self.nc.dram_tensor(name, list(shape), dt, kind=kind)
        if dbg:
            self.outs.append(name)
        return t

    def build(self):
        nc = self.nc
        dbg = self.debug
        I = {}
        I["xin"] = self.din("xin", [NTOK, D])
        I["cvec"] = self.din("cvec", [128, 8, 2])
        I["mod_w"] = self.din("mod_w", [2, D, 6 * D])
        I["mod_b"] = self.din("mod_b", [2, 6 * D])
        I["mod_bfm"] = self.din("mod_bfm", [2, 128, 48])
        I["n1g"] = self.din("n1g", [2, 128, 8])
        I["n2g"] = self.din("n2g", [2, 128, 8])
        I["fng"] = self.din("fng", [D])
        I["w0"] = self.din("w0", [D, 2432])
        I["w0o"] = self.din("w0o", [D, D])
        I["sink"] = self.din("sink", [8])
        I["dlam"] = self.din("dlam", [256])
        I["subg"] = self.din("subg", [128, 1])
        I["ctab"] = self.din("ctab", [128, NTOK])
        I["stab"] = self.din("stab", [128, NTOK])
        I["consts"] = self.din("consts", [128, NCOL])
        I["sel"] = self.din("sel", [128, 2])
        I["w1"] = self.din("w1", [D, 3072])
        I["gw1"] = self.din("gw1", [128, 8, 32])
        I["gw2"] = self.din("gw2", [32, 2, 512])
        I["glng"] = self.din("glng", [128, 2])
        I["w1o"] = self.din("w1o", [D, D])
        I["wup"] = self.din("wup", [2, D, 2 * DFF])
        I["cw"] = self.din("cw", [2, 128, 44, 3])
        I["cbias"] = self.din("cbias", [2, 128, 44])
        I["wdn"] = self.din("wdn", [2, DFF, D])
        self.I = I
        S = {}
        S["qkt"] = self.dscr("qkt", [14, 128, NTOK], BF16, dbg == "b1")
        S["vs"] = self.dscr("vs", [128, 34, 1024], BF16, dbg == "b1")
        S["hm"] = self.dscr("hm", [TO + L, D], F32, dbg == "b3")
        S["h1"] = self.dscr("h1", [TO + L, D], F32, dbg == "c")
        S["hm1"] = self.dscr("hm1", [TO, D], F32, dbg == "e")
        S["row"] = self.dscr("row", [1, D], F32)
        S["rowg"] = self.dscr("rowg", [2, D], F32)
        S["row2"] = self.dscr("row2", [1, D], F32)
        S["rowg2"] = self.dscr("rowg2", [2, D], F32)
        S["st"] = self.dscr("st", [128, 1024], F32)
        S["ofwd"] = self.dscr("ofwd", [8, 128, TO], F32)
        S["oall1"] = self.dscr("oall1", [8, 128, TO], BF16)
        S["stg"] = self.dscr("stg", [256, 1024], F32)
        if dbg == "b1":
            S["dbgmod"] = self.dscr("dbgmod", [128, 96 + 32], F32, True)
        self.S = S
        self.y = nc.dram_tensor("y", [TO, D], F32, kind="ExternalOutput")
        self.outs.append("y")
        self.By = Buf("y")

        with ExitStack() as st:
            self.k = K(nc, st)
            self.st = st
            self.glob()
            self.layer0()
            if dbg in ("b1", "b3", "c"):
                self.finish()
                return nc
            self.layer1()
            self.finish()
        return nc

    def finish(self):
        k = self.k
        k.barrier()

    def sb(self, st, name, shape, dt=F32):
        self.uid = getattr(self, "uid", 0) + 1
        return st.enter_context(self.nc.sbuf_tensor("s%d_%s" % (self.uid, name), list(shape), dt))

    def ps(self, st, name, shape=(128, 512), dt=F32):
        self.uid = getattr(self, "uid", 0) + 1
        return st.enter_context(self.nc.psum_tensor("p%d_%s" % (self.uid, name), list(shape), dt))

    def glob(self):
        k, I, st = self.k, self.I, self.st
        self.cF = self.sb(st, "cF", [128, NCOL])
        self.cB = self.sb(st, "cB", [128, NCOL], BF16)
        self.BcF, self.BcB = Buf("cF"), Buf("cB")
        k.dma("sp", self.cF[:], I["consts"].ap(), writes=[self.BcF])
        k.op("dve", lambda e: e.tensor_copy(out=self.cB[:], in_=self.cF[:]), [self.BcF], [self.BcB])
        self.sel = self.sb(st, "sel", [128, 2])
        self.Bsel = Buf("sel")
        k.dma("sp", self.sel[:], I["sel"].ap(), writes=[self.Bsel])
        self.silu = self.sb(st, "silu", [128, 8, 2])
        self.Bsilu = Buf("silu")
        cv = self.sb(st, "cv", [128, 8, 2])
        Bcv = Buf()
        k.dma("sp", cv[:], I["cvec"].ap(), writes=[Bcv])
        k.op("act", lambda e: e.activation(out=self.silu[:], in_=cv[:], func=AF.Silu), [Bcv], [self.Bsilu])
        self.modfm = self.sb(st, "modfm", [128, 48, 2])
        self.Bmodfm = Buf("modfm")
        self.ab = self.sb(st, "ab", [128, 2, 4, 8])
        self.Bab = Buf("ab")
        self.gbc = self.sb(st, "gbc", [128, 2, 2, D])
        self.Bgbc = Buf("gbc")
        self.epsT = self.sb(st, "epsT", [128, 1])
        self.Beps = Buf("eps")
        k.op("pool", lambda e: e.memset(self.epsT[:], EPS), [], [self.Beps])
        self.stat = self.sb(st, "stat", [128, 14, 9])
        self.Bstat = Buf("stat")

    def compute_mod(self, l, need_ctx):
        k, I = self.k, self.I
        nc = self.nc
        with ExitStack() as st:
            mw = [self.sb(st, "mw%d" % i, [128, 8, 512]) for i in range(2)]
            self.srep = self.sb(st, "srep", [128, 2, 8, 128])
            self.Bsrep = Buf("srep")
            for v in range(2):
                for kk in range(8):
                    k.op("act", lambda e: e.activation(out=self.srep[:, v, kk, :], in_=self.cF[:, C_ONE:C_ONE + 128],
                                                       func=AF.Copy, scale=self.silu[:, kk, v:v + 1]),
                         [self.Bsilu, self.BcF], [self.Bsrep])
            Bmw = [Buf(), Buf()]
            pm = self.ps(st, "pm", [128, 512])
            Bpm = Buf()
            pg = [self.ps(st, "pg%d" % i, [128, 512]) for i in range(2)]
            Bpg = [Buf(), Buf()]
            mbb = self.sb(st, "mbb", [128, 512])
            Bmbb = Buf()
            mbf = self.sb(st, "mbf", [128, 48])
            Bmbf = Buf()
            ng = self.sb(st, "ng", [128, 2, 8])
            Bng = Buf()
            k.dma("sp", mbf[:], I["mod_bfm"].ap()[l], writes=[Bmbf])
            k.dma("sp", ng[:, 0, :], I["n1g"].ap()[l], writes=[Bng])
            k.dma("sp", ng[:, 1, :], I["n2g"].ap()[l], writes=[Bng])
            it = 0
            for cb in range(12):
                j = it % 2
                it += 1
                src = I["mod_w"].ap()[l, :, cb * 512:(cb + 1) * 512].rearrange("(k p) n -> p k n", p=128)
                k.dma("sp", mw[j][:], src, writes=[Bmw[j]])
                if cb in (4, 5, 10, 11):
                    gi = 0 if cb < 6 else 1
                    half = cb % 2 if cb < 6 else (cb - 10)
                    k.dma("sp", mbb[:], dap(I["mod_b"], l * 6 * D + cb * 512, [[0, 128], [1, 512]]), writes=[Bmbb])
                    for v in range(2 if need_ctx else 1):
                        p = pg[v]
                        for kk in range(8):
                            k.op("pe", lambda e: e.matmul(p[:], lhsT=self.srep[:, v, kk, :], rhs=mw[j][:, kk, :],
                                                          start=(kk == 0), stop=(kk == 7)),
                                 [self.Bsrep, Bmw[j]], [Bpg[v]])
                        k.op("dve", lambda e: e.tensor_tensor(out=self.gbc[:, v, gi, half * 512:(half + 1) * 512],
                                                              in0=p[:], in1=mbb[:], op=ALU.add),
                             [Bpg[v], Bmbb], [self.Bgbc])
                else:
                    for jj in range(4):
                        ch = cb * 4 + jj
                        for kk in range(8):
                            k.op("pe", lambda e: e.matmul(pm[:, ch * 2:ch * 2 + 2], lhsT=mw[j][:, kk, jj * 128:(jj + 1) * 128],
                                                          rhs=self.silu[:, kk, :], start=(kk == 0), stop=(kk == 7)),
                                 [self.Bsilu, Bmw[j]], [Bpm])
            for v in range(2):
                k.op("dve", lambda e: e.tensor_tensor(out=self.modfm[:, :, v],
                                                      in0=pm[:, 0:96].rearrange("p (c v) -> p c v", v=2)[:, :, v],
                                                      in1=mbf[:], op=ALU.add), [Bpm, Bmbf], [self.Bmodfm])
            for v in range(2):
                for (ai, sc0, sh0, gi) in ((0, 8, 0, 0), (2, 32, 24, 1)):
                    k.op("dve", lambda e: e.scalar_tensor_tensor(out=self.ab[:, v, ai, :], in0=self.modfm[:, sc0:sc0 + 8, v],
                                                                 scalar=1.0, in1=ng[:, gi, :], op0=ALU.add, op1=ALU.mult),
                         [self.Bmodfm, Bng], [self.Bab])
                    k.op("dve", lambda e: e.tensor_copy(out=self.ab[:, v, ai + 1, :], in_=self.modfm[:, sh0:sh0 + 8, v]),
                         [self.Bmodfm], [self.Bab])
            k.barrier()

    def norm_a(self, R, srcs, v, which, X, BX, col0, rd=()):
        k = self.k
        i = R["i"] % R["n"]
        ip = R["i"] % R["npt"]
        R["i"] += 1
        xt, Bxt = R["xt"][i], R["Bxt"][i]
        nrows = 0
        for ap, r0 in srcs:
            n = ap.shape[0]
            k.dma("sp", xt[r0:r0 + n, :], ap, reads=list(rd), writes=[Bxt])
            nrows = max(nrows, r0 + n)
        ss, Bss = R["ss"][i], R["Bss"][i]
        junk, Bjunk = R["junk"], R["Bjunk"]
        k.op(R.get("ms_eng", "pool"), lambda e: e.memset(ss[:], 0.0), [], [Bss])
        k.op("act", lambda e: e.activation(out=junk[:nrows, :], in_=xt[:nrows, :], func=AF.Square,
                                           accum_out=ss[:nrows, 0:1]), [Bxt, Bss], [Bjunk, Bss])
        k.op("act", lambda e: e.activation(out=ss[:nrows, 1:2], in_=ss[:nrows, 0:1], func=AF.Ln, scale=1.0 / D,
                                           bias=self.epsT[:nrows, 0:1]), [Bss, self.Beps], [Bss])
        k.op("act", lambda e: e.activation(out=ss[:nrows, 2:3], in_=ss[:nrows, 1:2], func=AF.Exp, scale=-0.5), [Bss], [Bss])
        xh, Bxh = R["xh"][i], R["Bxh"][i]
        k.op("act", lambda e: e.activation(out=xh[:nrows, :], in_=xt[:nrows, :], func=AF.Copy, scale=ss[:nrows, 2:3]),
             [Bxt, Bss], [Bxh])
        return (i, nrows, v, which, X, BX, col0, ip)

    def norm_b(self, R, cx):
        k = self.k
        i, nrows, v, which, X, BX, col0, ip = cx
        xt, Bxt = R["xt"][i], R["Bxt"][i]
        xh, Bxh = R["xh"][i], R["Bxh"][i]
        pt, Bpt = R["pt"][ip], R["Bpt"][ip]
        for kk in range(8):
            k.op("pe", lambda e: e.transpose(out=pt[:, kk * 128:kk * 128 + nrows], in_=xh[:nrows, kk * 128:(kk + 1) * 128],
                                             identity=self.cB[:nrows, C_ID:C_ID + nrows]), [Bxh, self.BcB], [Bpt])
        a_bc = self.ab[:, v, 2 * which, :].unsqueeze(2).to_broadcast([128, 8, nrows])
        b_bc = self.ab[:, v, 2 * which + 1, :].unsqueeze(2).to_broadcast([128, 8, nrows])
        ptv = pt[:].rearrange("p (k t) -> p k t", t=128)[:, :, :nrows]
        tmpv = xt[:].rearrange("p (k t) -> p k t", t=128)[:, :, :nrows]
        k.op("dve", lambda e: e.tensor_tensor(out=tmpv, in0=ptv, in1=a_bc, op=ALU.mult), [Bpt, self.Bab, Bxh], [Bxt])
        k.op("dve", lambda e: e.tensor_tensor(out=X[:, :, col0:col0 + nrows], in0=tmpv, in1=b_bc, op=ALU.add), [Bxt, self.Bab], [BX])

    def norm_block(self, R, srcs, v, which, X, BX, col0, rd=()):
        self.norm_b(R, self.norm_a(R, srcs, v, which, X, BX, col0, rd))

    def norm_seq(self, R, items):
        prev = None
        for it in items:
            cx = self.norm_a(R, *it)
            if prev is not None:
                self.norm_b(R, prev)
            prev = cx
        if prev is not None:
            self.norm_b(R, prev)

    def norm_res(self, st, nbuf=2, npt=2):
        R = {"i": 0, "n": nbuf, "npt": npt}
        R["xt"] = [self.sb(st, "nb_xt%d" % i, [128, D]) for i in range(nbuf)]
        R["Bxt"] = [Buf() for _ in range(nbuf)]
        R["ss"] = [self.sb(st, "nb_ss%d" % i, [128, 4]) for i in range(nbuf)]
        R["Bss"] = [Buf() for _ in range(nbuf)]
        R["junk"] = self.sb(st, "nb_junk", [128, D], BF16)
        R["Bjunk"] = Buf()
        R["xh"] = [self.sb(st, "nb_xh%d" % i, [128, D], BF16) for i in range(nbuf)]
        R["Bxh"] = [Buf() for _ in range(nbuf)]
        R["pt"] = [self.ps(st, "nb_pt%d" % i, [128, D], BF16) for i in range(npt)]
        R["Bpt"] = [Buf() for _ in range(npt)]
        return R

    def layer0(self):
        k, I = self.k, self.I
        sw = ExitStack()
        W = self.sb(sw, "b1_W", [128, 8, 2432], BF16)
        BW = Buf()
        for g0 in range(0, 2432, 512):
            g1 = min(2432, g0 + 512)
            k.dma("pool", W[:, :, g0:g1], I["w0"].ap()[:, g0:g1].rearrange("(k p) n -> p k n", p=128), writes=[BW])
        self.compute_mod(0, True)
        if self.debug == "b1":
            k = self.k
            k.dma("sp", self.S["dbgmod"].ap()[:, 0:96], self.modfm[:].rearrange("p c v -> p (c v)"), reads=[self.Bmodfm])
            k.dma("sp", self.S["dbgmod"].ap()[:, 96:128], self.ab[:, 0, :, :].rearrange("p a k -> p (a k)"), reads=[self.Bab])
        self.phase_b1(W, BW)
        sw.close()
        if self.debug == "b1":
            return
        self.phase_b2()
        if self.debug == "b3":
            return
        self.ffn(0)

    def phase_b1(self, W, BW):
        k, I, S, nc = self.k, self.I, self.S, self.nc
        with ExitStack() as st:
            R = self.norm_res(st, nbuf=3, npt=2)
            R["ms_eng"] = "dve"
            X = [self.sb(st, "b1_X%d" % i, [128, 8, 512], BF16) for i in range(2)]
            BX = [Buf(), Buf()]
            pp = [self.ps(st, "b1_pp%d" % i) for i in range(2)]
            Bpp = [Buf(), Buf()]
            pr = self.ps(st, "b1_pr")
            Bpr = Buf()
            pq = self.ps(st, "b1_pq")
            Bpq = Buf()
            pv = [self.ps(st, "b1_pv%d" % i) for i in range(2)]
            Bpv = [Buf(), Buf()]
            tab = [self.sb(st, "b1_tab%d" % i, [128, 2, 512]) for i in range(2)]
            Btab = [Buf(), Buf()]
            qsb = [self.sb(st, "b1_qsb%d" % i, [128, 512], BF16) for i in range(2)]
            Bqsb = [Buf(), Buf()]
            sq = [self.sb(st, "b1_sq%d" % i, [128, 512], BF16) for i in range(2)]
            Bsq = [Buf(), Buf()]
            t1 = [self.sb(st, "b1_t1%d" % i, [128, 512]) for i in range(2)]
            Bt1 = [Buf(), Buf()]
            t2 = [self.sb(st, "b1_t2%d" % i, [128, 512]) for i in range(2)]
            Bt2 = [Buf(), Buf()]
            ro = [self.sb(st, "b1_ro%d" % i, [128, 512], BF16) for i in range(3)]
            Bro = [Buf(), Buf(), Buf()]
            vo = [self.sb(st, "b1_vo%d" % i, [128, 1024], BF16) for i in range(2)]
            Bvo = [Buf(), Buf()]
            k.op("pool", lambda e: e.memset(self.stat[:], 0.0), [], [self.Bstat])
            for j in range(2):
                k.op("pool", lambda e: e.memset(vo[j][:], 0.0), [], [Bvo[j]])
            Bqkt, Bvs = Buf("qkt"), Buf("vs")
            self.Bqkt, self.Bvs = Bqkt, Bvs
            cnt = 0
            vcnt = 0
            for ti in range(9):
                n = 512 if ti < 8 else 256
                c0 = ti * 512
                xi = ti % 2
                v = 0 if ti < 8 else 1
                self.norm_seq(R, [([(I["xin"].ap()[c0 + b * 128:c0 + (b + 1) * 128, :], 0)], v, 0, X[xi], BX[xi], b * 128)
                                  for b in range(n // 128)])
                k.dma("sp", tab[xi][:, 0, :n], I["ctab"].ap()[:, c0:c0 + n], writes=[Btab[xi]])
                k.dma("sp", tab[xi][:, 1, :n], I["stab"].ap()[:, c0:c0 + n], writes=[Btab[xi]])
                if ti < 4 or ti == 8:
                    chunks = list(range(14))
                elif ti == 4:
                    chunks = [4, 5, 10, 11, 12, 13]
                else:
                    chunks = [10, 11, 12, 13]
                def st_a(ci_):
                    ch = chunks[ci_]
                    j = (cnt + ci_) % 2
                    p = pp[j]
                    for kk in range(8):
                        k.op("pe", lambda e: e.matmul(p[:, :n], lhsT=W[:, kk, ch * 128:(ch + 1) * 128], rhs=X[xi][:, kk, :n],
                                                      start=(kk == 0), stop=(kk == 7)), [BW, BX[xi]], [Bpp[j]])
                    k.op("act", lambda e: e.activation(out=sq[j][:, :n], in_=p[:, :n], func=AF.Square), [Bpp[j]], [Bsq[j]])
                    k.op("act", lambda e: e.activation(out=qsb[j][:, :n], in_=p[:, :n], func=AF.Copy), [Bpp[j]], [Bqsb[j]])

                def st_b(ci_):
                    ch = chunks[ci_]
                    j = (cnt + ci_) % 2
                    j3 = (cnt + ci_) % 3
                    p = pp[j]
                    k.op("pe", lambda e: e.matmul(pq[:, :n], lhsT=self.cB[:, C_BD:C_BD + 128], rhs=sq[j][:, :n],
                                                  start=True, stop=True), [self.BcB, Bsq[j]], [Bpq])
                    k.op("dve", lambda e: e.tensor_reduce(out=self.stat[:, ch, ti:ti + 1], in_=pq[:, :n], axis=AX.X, op=ALU.max),
                         [Bpq], [self.Bstat])
                    k.op("pe", lambda e: e.matmul(pr[:, :n], lhsT=self.cB[:, C_PERM:C_PERM + 128], rhs=qsb[j][:, :n],
                                                  start=True, stop=True), [self.BcB, Bqsb[j]], [Bpr])
                    k.op("dve", lambda e: e.tensor_tensor(out=t1[j][:, :n], in0=p[:, :n], in1=tab[xi][:, 0, :n], op=ALU.mult),
                         [Bpp[j], Btab[xi]], [Bt1[j]])
                    k.op("dve", lambda e: e.tensor_tensor(out=t2[j][:, :n], in0=pr[:, :n], in1=tab[xi][:, 1, :n], op=ALU.mult),
                         [Bpr, Btab[xi]], [Bt2[j]])
                    k.op("dve", lambda e: e.tensor_tensor(out=ro[j3][:, :n], in0=t1[j][:, :n], in1=t2[j][:, :n], op=ALU.add),
                         [Bt1[j], Bt2[j]], [Bro[j3]])
                    k.dma("sp", S["qkt"].ap()[ch, :, c0:c0 + n], ro[j3][:, :n], reads=[Bro[j3]], writes=[Bqkt])

                for ci_ in range(len(chunks) + 1):
                    if ci_ < len(chunks):
                        st_a(ci_)
                    if ci_ >= 1:
                        st_b(ci_ - 1)
                cnt += len(chunks)
                for b in range(n // 128):
                    j = vcnt % 2
                    vcnt += 1
                    blk = (c0 + b * 128) // 128
                    for kk in range(8):
                        k.op("pe", lambda e: e.matmul(pv[0][:, :], lhsT=X[xi][:, kk, b * 128:(b + 1) * 128], rhs=W[:, kk, 1792:2304],
                                                      start=(kk == 0), stop=(kk == 7)), [BW, BX[xi]], [Bpv[0]])
                    for kk in range(8):
                        k.op("pe", lambda e: e.matmul(pv[1][:, :128], lhsT=X[xi][:, kk, b * 128:(b + 1) * 128], rhs=W[:, kk, 2304:2432],
                                                      start=(kk == 0), stop=(kk == 7)), [BW, BX[xi]], [Bpv[1]])
                    k.op("act", lambda e: e.activation(out=vo[j][:, 0:512], in_=pv[0][:, :], func=AF.Copy), [Bpv[0]], [Bvo[j]])
                    k.op("dve", lambda e: e.tensor_copy(out=vo[j][:, 512:1024].rearrange("p (a c) -> p a c", c=256)[:, :, 0:64],
                                                        in_=pv[1][:, 0:128].rearrange("p (a c) -> p a c", c=64)), [Bpv[1]], [Bvo[j]])
                    k.op("dve", lambda e: e.tensor_copy(out=vo[j][:, 512:1024].rearrange("p (a c) -> p a c", c=256)[:, :, 192:256],
                                                        in_=pv[1][:, 0:128].rearrange("p (a c) -> p a c", c=64)), [Bpv[1]], [Bvo[j]])
                    k.dma("sp", S["vs"].ap()[:, blk, :], vo[j][:], reads=[Bvo[j]], writes=[Bvs])
            k.barrier()

    def phase_b2(self):
        k, I, S, nc = self.k, self.I, self.S, self.nc
        cB, cF = self.cB, self.cF
        NQ = TO + L
        with ExitStack() as so:
            Oall = self.sb(so, "Oall", [128, 8, NQ], BF16)
            BO = Buf("Oall")
            wo_pre = self.load_wo(so, I["w0o"], defer=True)
            sm = self.sb(so, "sm", [128, 2, 14])
            negm = self.sb(so, "negm", [128, 2, 2, 4])
            skv = self.sb(so, "skv", [128, 4])
            lamv = self.sb(so, "lamv", [128, 4])
            sgs = self.sb(so, "sgs", [128, 1])
            Bsm, Bnegm, Bskv, Blam, Bsgs = Buf(), Buf(), Buf(), Buf(), Buf()
            with ExitStack() as st:
                smax = self.sb(st, "smax", [128, 14])
                Bsmax = Buf()
                pst = self.ps(st, "pst")
                Bpst = Buf()
                k.op("dve", lambda e: e.tensor_reduce(out=smax[:], in_=self.stat[:], axis=AX.X, op=ALU.max), [self.Bstat], [Bsmax])
                k.op("pe", lambda e: e.matmul(pst[:, 0:14], lhsT=cF[:, C_ELO:C_ELO + 128], rhs=smax[:], start=True, stop=True),
                     [self.BcF, Bsmax], [Bpst])
                k.op("pe", lambda e: e.matmul(pst[:, 14:28], lhsT=cF[:, C_EHI:C_EHI + 128], rhs=smax[:], start=True, stop=True),
                     [self.BcF, Bsmax], [Bpst])
                k.op("dve", lambda e: e.tensor_copy(out=sm[:].rearrange("p a c -> p (a c)"), in_=pst[:, 0:28]), [Bpst], [Bsm])
                m2 = self.sb(st, "m2", [128, 2, 2, 4])
                Bm2 = Buf()
                for j in range(2):
                    for kv in range(2):
                        k.op("dve", lambda e: e.tensor_scalar(out=m2[:, 0, j, 2 * kv:2 * kv + 2], in0=sm[:, j, 2 * kv:2 * kv + 2],
                                                              scalar1=sm[:, j, 4 + kv:5 + kv], scalar2=None, op0=ALU.mult), [Bsm], [Bm2])
                    k.op("dve", lambda e: e.tensor_tensor(out=m2[:, 1, j, :], in0=sm[:, j, 6:10], in1=sm[:, j, 10:14], op=ALU.mult),
                         [Bsm], [Bm2])
                lnm = self.sb(st, "lnm", [128, 16])
                Blnm = Buf()
                k.op("act", lambda e: e.activation(out=lnm[:], in_=m2[:].rearrange("p a b c -> p (a b c)"), func=AF.Ln), [Bm2], [Blnm])
                k.op("act", lambda e: e.activation(out=lnm[:], in_=lnm[:], func=AF.Exp, scale=0.5), [Blnm], [Blnm])
                k.op("dve", lambda e: e.tensor_scalar(out=negm[:].rearrange("p a b c -> p (a b c)"), in0=lnm[:], scalar1=-SCALE,
                                                      scalar2=None, op0=ALU.mult), [Blnm], [Bnegm])
                sk = self.sb(st, "sk", [128, 8])
                Bsk = Buf()
                k.dma("sp", sk[:], dap(I["sink"], 0, [[0, 128], [1, 8]]), writes=[Bsk])
                esk = self.sb(st, "esk", [128, 2, 4])
                Besk = Buf()
                k.op("dve", lambda e: e.tensor_tensor(out=esk[:], in0=sk[:].rearrange("p (c j) -> p j c", j=2), in1=negm[:, 0, :, :],
                                                      op=ALU.add), [Bsk, Bnegm], [Besk])
                k.op("act", lambda e: e.activation(out=esk[:], in_=esk[:], func=AF.Exp), [Besk], [Besk])
                k.op("dve", lambda e: e.tensor_copy(out=skv[0:64, :], in_=esk[0:64, 0, :]), [Besk], [Bskv])
                k.op("dve", lambda e: e.tensor_copy(out=skv[64:128, :], in_=esk[64:128, 1, :]), [Besk], [Bskv])
                dl = self.sb(st, "dl", [128, 256])
                Bdl = Buf()
                k.dma("sp", dl[:], dap(I["dlam"], 0, [[0, 128], [1, 256]]), writes=[Bdl])
                dp = self.sb(st, "dp", [128, 2, 64])
                Bdp = Buf()
                dlv = dl[:].rearrange("p (a b d) -> p a b d", a=2, b=2)
                k.op("dve", lambda e: e.tensor_tensor(out=dp[:], in0=dlv[:, :, 0, :], in1=dlv[:, :, 1, :], op=ALU.mult), [Bdl], [Bdp])
                k.op("dve", lambda e: e.tensor_reduce(out=lamv[:, 0:2], in_=dp[:], axis=AX.X, op=ALU.add), [Bdp], [Blam])
                k.op("act", lambda e: e.activation(out=lamv[:, 0:2], in_=lamv[:, 0:2], func=AF.Exp), [Blam], [Blam])
                k.op("dve", lambda e: e.tensor_tensor(out=lamv[:, 2:3], in0=lamv[:, 1:2], in1=lamv[:, 0:1], op=ALU.subtract), [Blam], [Blam])
                k.op("dve", lambda e: e.tensor_scalar(out=lamv[:, 3:4], in0=lamv[:, 2:3], scalar1=-LAM0, scalar2=None, op0=ALU.add),
                     [Blam], [Blam])
                sg0 = self.sb(st, "sg0", [128, 1])
                Bsg0 = Buf()
                k.dma("sp", sg0[:], I["subg"].ap(), writes=[Bsg0])
                k.op("dve", lambda e: e.tensor_scalar(out=sgs[:], in0=sg0[:], scalar1=(1.0 - LAM0), scalar2=None, op0=ALU.mult),
                     [Bsg0], [Bsgs])
                k.barrier()
            with ExitStack() as st:
                kaT = self.sb(st, "kaT", [128, 2, 2432], BF16)
                qaT = self.sb(st, "qaT", [128, 4, NQ], BF16)
                vaT = self.sb(st, "vaT", [128, 19, 512], BF16)
                Bka, Bqa, Bva = Buf(), Buf(), Buf()
                for kv in range(2):
                    k.dma("sp", kaT[:, kv, 0:2176], S["qkt"].ap()[4 + kv, :, 0:2176], reads=[self.Bqkt], writes=[Bka])
                    k.dma("sp", kaT[:, kv, 2176:2432], S["qkt"].ap()[4 + kv, :, T:NTOK], reads=[self.Bqkt], writes=[Bka])
                for c in range(4):
                    k.dma("sp", qaT[:, c, 0:TO], S["qkt"].ap()[c, :, 0:TO], reads=[self.Bqkt], writes=[Bqa])
                    k.dma("sp", qaT[:, c, TO:NQ], S["qkt"].ap()[c, :, T:NTOK], reads=[self.Bqkt], writes=[Bqa])
                k.dma("sp", vaT[:, 0:17, :], S["vs"].ap()[:, 0:17, 512:1024], reads=[self.Bvs], writes=[Bva])
                k.dma("sp", vaT[:, 17:19, :], S["vs"].ap()[:, 32:34, 512:1024], reads=[self.Bvs], writes=[Bva])
                pss = [self.ps(st, "a_ps%d" % i) for i in range(4)]
                Bpss = [Buf() for _ in range(4)]
                po = self.ps(st, "a_po")
                pd = self.ps(st, "a_pd")
                Bpo, Bpd = Buf(), Buf()
                ptl = [self.sb(st, "a_pt%d" % i, [128, 512], BF16) for i in range(4)]
                Bptl = [Buf() for _ in range(4)]
                den = self.sb(st, "a_den", [128, 512])
                Bden = Buf()
                cnt = 0
                for qi in range(5):
                    nq = 512 if qi < 4 else 256
                    qc0 = qi * 512
                    chunks = [(2176 + 128 * cc, 17 + cc, 0, nq, None) for cc in range(2)]
                    if qi < 4:
                        spans = [(0, 128, 384), (0, 256, 256), (0, 384, 128), (128, 384, 128), (256, 256, 128), (384, 128, 128)]
                        for j6 in range(6):
                            kc = qc0 - 128 + 128 * j6
                            if kc < 0:
                                continue
                            s0, ln, u0 = spans[j6]
                            chunks.append((kc, kc // 128, s0, ln, u0))
                    for c in range(4):
                        kv = c // 2
                        steps = [(j, ch) for j in range(2) for ch in chunks]
                        nst = len(steps)
                        base = cnt
                        cnt += nst

                        def qk_step(si):
                            j, (kcol, vblk, s0, ln, u0) = steps[si]
                            i3 = (base + si) % 4
                            r0 = 64 * j
                            k.op("pe", lambda e: e.matmul(pss[i3][:, :ln], lhsT=kaT[r0:r0 + 64, kv, kcol:kcol + 128],
                                                          rhs=qaT[r0:r0 + 64, c, qc0 + s0:qc0 + s0 + ln], start=True, stop=True),
                                 [Bka, Bqa], [Bpss[i3]])
                            k.op("act", lambda e: e.activation(out=ptl[i3][:, :ln], in_=pss[i3][:, :ln], func=AF.Exp, scale=SCALE,
                                                               bias=negm[:, 0, j, c:c + 1]), [Bpss[i3], Bnegm], [Bptl[i3]])
                            if u0 is not None:
                                k.op("dve", lambda e: e.tensor_tensor(out=ptl[i3][:, :ln], in0=ptl[i3][:, :ln],
                                                                      in1=cB[:, C_BAND + u0:C_BAND + u0 + ln], op=ALU.mult),
                                     [Bptl[i3], self.BcB], [Bptl[i3]])

                        def pv_step(si):
                            j, (kcol, vblk, s0, ln, u0) = steps[si]
                            i3 = (base + si) % 4
                            first = (si == 0)
                            vc = (2 * kv + j) * 128
                            k.op("pe", lambda e: e.matmul(po[:, s0:s0 + ln], lhsT=vaT[:, vblk, vc:vc + 128], rhs=ptl[i3][:, :ln],
                                                          start=first, stop=False, skip_group_check=True), [Bva, Bptl[i3]], [Bpo])
                            oc = C_OLO if j == 0 else C_OHI
                            k.op("pe", lambda e: e.matmul(pd[:, s0:s0 + ln], lhsT=cB[:, oc:oc + 128], rhs=ptl[i3][:, :ln],
                                                          start=first, stop=False, skip_group_check=True), [self.BcB, Bptl[i3]], [Bpd])
                        for t2 in range(0, nst + 2, 2):
                            for si in (t2, t2 + 1):
                                if si < nst:
                                    qk_step(si)
                            for si in (t2 - 2, t2 - 1):
                                if 0 <= si < nst:
                                    pv_step(si)
                        k.op("act", lambda e: e.activation(out=den[:, :nq], in_=pd[:, :nq], func=AF.Ln, bias=skv[:, c:c + 1]),
                             [Bpd, Bskv], [Bden])
                        k.op("act", lambda e: e.activation(out=den[:, :nq], in_=den[:, :nq], func=AF.Exp, scale=-1.0), [Bden], [Bden])
                        k.op("dve", lambda e: e.tensor_tensor(out=Oall[:, c, qc0:qc0 + nq], in0=po[:, :nq], in1=den[:, :nq], op=ALU.mult),
                             [Bpo, Bden], [BO])
                k.barrier()
            self._wo_issue()
            with ExitStack() as st:
                kbT = [self.sb(st, "kbT%d" % i, [128, NTOK], BF16) for i in range(2)]
                qbT = [self.sb(st, "qbT%d" % i, [128, NQ], BF16) for i in range(2)]
                vb = [self.sb(st, "vb%d" % i, [128, 34, 128], BF16) for i in range(2)]
                Bkb, Bqb, Bvb = [Buf(), Buf()], [Buf(), Buf()], [Buf(), Buf()]
                pss = [self.ps(st, "b_ps%d" % i, [128, 1024]) for i in range(2)]
                Bpss = [Buf() for _ in range(2)]
                po = [self.ps(st, "b_po%d" % i) for i in range(2)]
                Bpo = [Buf(), Buf()]
                pf = [self.ps(st, "b_pf%d" % i) for i in range(2)]
                Bpf = [Buf(), Buf()]
                ptl = [self.sb(st, "b_pt%d" % i, [128, 1024], BF16) for i in range(3)]
                Bptl = [Buf() for _ in range(3)]
                acc2 = self.sb(st, "b_acc", [128, 1024])
                Bacc2 = Buf()
                acc = [acc2[:, 0:512], acc2[:, 512:1024]]
                Bacc = [Bacc2, Bacc2]
                r = [self.sb(st, "b_r%d" % i, [128, 512]) for i in range(2)]
                Br = [Buf(), Buf()]
                tt = [self.sb(st, "b_t%d" % i, [128, 512]) for i in range(2)]
                Btt = [Buf(), Buf()]
                od = self.sb(st, "b_od", [128, 512])
                Bod = Buf()
                sqd = self.sb(st, "b_sqd", [128, 512], BF16)
                Bsqd = Buf()
                negmB = self.sb(st, "b_negm", [128, 4])
                BnegmB = Buf()
                k.op("dve", lambda e: e.tensor_tensor(out=negmB[:], in0=negm[:, 1, 0, :], in1=negm[:, 1, 1, :], op=ALU.min), [Bnegm], [BnegmB])
                cnt = 0
                for h in range(4):
                    hb = h % 2
                    k.dma("sp", kbT[hb][:], S["qkt"].ap()[10 + h], reads=[self.Bqkt], writes=[Bkb[hb]])
                    k.dma("sp", qbT[hb][:, 0:TO], S["qkt"].ap()[6 + h, :, 0:TO], reads=[self.Bqkt], writes=[Bqb[hb]])
                    k.dma("sp", qbT[hb][:, TO:NQ], S["qkt"].ap()[6 + h, :, T:NTOK], reads=[self.Bqkt], writes=[Bqb[hb]])
                    k.dma("sp", vb[hb][:], S["vs"].ap()[:, :, h * 128:(h + 1) * 128], reads=[self.Bvs], writes=[Bvb[hb]])
                    for qi in range(5):
                        nq = 512 if qi < 4 else 256
                        qc0 = qi * 512
                        chunks = list(range(34)) if qi < 4 else [32, 33]
                        nch = len(chunks)
                        base = cnt
                        cnt += nch

                        def qk_pair(ci):
                            kc = chunks[ci]
                            i2 = (base + ci) % 2
                            i3 = (base + ci) % 3
                            for m in range(2):
                                r0 = 64 * m
                                k.op("pe", lambda e: e.matmul(pss[i2][:, m * 512:m * 512 + nq], lhsT=kbT[hb][r0:r0 + 64, kc * 128:(kc + 1) * 128],
                                                              rhs=qbT[hb][r0:r0 + 64, qc0:qc0 + nq], start=True, stop=True),
                                     [Bkb[hb], Bqb[hb]], [Bpss[i2]])
                            k.op("act", lambda e: e.activation(out=ptl[i3][:].rearrange("p (m n) -> p m n", m=2)[:, :, :nq],
                                                               in_=pss[i2][:].rearrange("p (m n) -> p m n", m=2)[:, :, :nq],
                                                               func=AF.Exp, scale=SCALE, bias=negmB[:, h:h + 1]), [Bpss[i2], BnegmB], [Bptl[i3]])

                        def pv_pair(ci):
                            kc = chunks[ci]
                            i3 = (base + ci) % 3
                            for m in range(2):
                                k.op("pe", lambda e: e.matmul(po[m][:, :nq], lhsT=vb[hb][:, kc, :], rhs=ptl[i3][:, m * 512:m * 512 + nq],
                                                              start=(ci == 0), stop=(ci == nch - 1)), [Bvb[hb], Bptl[i3]], [Bpo[m]])
                            av = acc2[:].rearrange("p (m n) -> p m n", m=2)[:, :, :nq]
                            pv_ = ptl[i3][:].rearrange("p (m n) -> p m n", m=2)[:, :, :nq]
                            if ci == 0:
                                k.op("dve", lambda e: e.tensor_copy(out=av, in_=pv_), [Bptl[i3]], [Bacc2])
                            elif ci % 3 == 2:
                                for m in range(2):
                                    k.op("pe", lambda e: e.matmul(pf[m][:, :nq], lhsT=cB[:, C_ONE:C_ONE + 128], rhs=ptl[i3][:, m * 512:m * 512 + nq],
                                                                  start=(ci == 2), stop=False), [self.BcB, Bptl[i3]], [Bpf[m]])
                            else:
                                k.op("dve", lambda e: e.tensor_tensor(out=av, in0=av, in1=pv_, op=ALU.add), [Bptl[i3], Bacc2], [Bacc2])
                        for ci in range(nch + 1):
                            if ci < nch:
                                qk_pair(ci)
                            if ci - 1 >= 0:
                                pv_pair(ci - 1)
                        for m in range(2):
                            k.op("pe", lambda e: e.matmul(pf[m][:, :nq], lhsT=cF[:, C_ONE:C_ONE + 128], rhs=acc[m][:, :nq],
                                                          start=(nch < 3), stop=True), [self.BcF, Bacc[m]], [Bpf[m]])
                            k.op("act", lambda e: e.activation(out=r[m][:, :nq], in_=pf[m][:, :nq], func=AF.Ln), [Bpf[m]], [Br[m]])
                            k.op("act", lambda e: e.activation(out=r[m][:, :nq], in_=r[m][:, :nq], func=AF.Exp, scale=-1.0), [Br[m]], [Br[m]])
                            k.op("dve", lambda e: e.tensor_tensor(out=tt[m][:, :nq], in0=po[m][:, :nq], in1=r[m][:, :nq], op=ALU.mult),
                                 [Bpo[m], Br[m]], [Btt[m]])
                        k.op("dve", lambda e: e.scalar_tensor_tensor(out=od[:, :nq], in0=tt[1][:, :nq], scalar=lamv[:, 3:4], in1=tt[0][:, :nq],
                                                                     op0=ALU.mult, op1=ALU.add), [Btt[0], Btt[1], Blam], [Bod])
                        k.op("act", lambda e: e.activation(out=sqd[:, :nq], in_=od[:, :nq], func=AF.Square), [Bod], [Bsqd])
                        k.op("pe", lambda e: e.matmul(pf[0][:, :nq], lhsT=cB[:, C_ONE:C_ONE + 128], rhs=sqd[:, :nq], start=True, stop=True),
                             [self.BcB, Bsqd], [Bpf[0]])
                        k.op("act", lambda e: e.activation(out=r[0][:, :nq], in_=pf[0][:, :nq], func=AF.Ln, scale=1.0 / 128,
                                                           bias=self.epsT[:, 0:1]), [Bpf[0], self.Beps], [Br[0]])
                        k.op("act", lambda e: e.activation(out=r[0][:, :nq], in_=r[0][:, :nq], func=AF.Exp, scale=-0.5), [Br[0]], [Br[0]])
                        k.op("dve", lambda e: e.scalar_tensor_tensor(out=Oall[:, 4 + h, qc0:qc0 + nq], in0=od[:, :nq], scalar=sgs[:, 0:1],
                                                                     in1=r[0][:, :nq], op0=ALU.mult, op1=ALU.mult), [Bod, Br[0], Bsgs], [BO])
                k.barrier()
            self.outproj(Oall, BO, I["w0o"], 0, [(I["xin"].ap()[b * 128:(b + 1) * 128, :], S["hm"].ap()[b * 128:(b + 1) * 128, :], 0, b * 128)
                                                 for b in range(16)] +
                         [(I["xin"].ap()[T + b * 128:T + (b + 1) * 128, :], S["hm"].ap()[TO + b * 128:TO + (b + 1) * 128, :], 1, TO + b * 128)
                          for b in range(2)], "hm", exch=(S["hm"], S["row"], S["rowg"], S["row2"]), pre=wo_pre)

    def load_wo(self, st, w_dram, defer=False):
        k = self.k
        wo = self.sb(st, "op_wo", [128, 8, D], BF16)
        Bwo = Buf()

        def issue():
            for kk in range(8):
                k.dma("pool", wo[:, kk, :], w_dram.ap()[kk * 128:(kk + 1) * 128, :], writes=[Bwo])
        if defer:
            self._wo_issue = issue
        else:
            issue()
        return wo, Bwo

    def outproj(self, Oall, BO, w_dram, gi, blocks, outname, xrd=(), exch=None, pre=None):
        k = self.k
        with ExitStack() as st:
            if pre is None:
                wo, Bwo = self.load_wo(st, w_dram)
            else:
                wo, Bwo = pre
            py = [self.ps(st, "op_py%d" % i) for i in range(4)]
            Bpy = [Buf() for _ in range(4)]
            xt = [self.sb(st, "op_xt%d" % i, [128, D]) for i in range(2)]
            Bxt = [Buf(), Buf()]
            t = [self.sb(st, "op_t%d" % i, [128, D]) for i in range(2)]
            Bt = [Buf(), Buf()]
            Bdst = getattr(self, "Bd_" + outname, None)
            if Bdst is None:
                Bdst = Buf(outname)
                setattr(self, "Bd_" + outname, Bdst)
            if exch is not None:
                blocks = [blocks[15]] + blocks[:15] + blocks[16:]
            for bi, (src, dst, v, col0) in enumerate(blocks):
                j = bi % 2
                k.dma("sp", xt[j][:], src, reads=list(xrd), writes=[Bxt[j]])
                for half in range(2):
                    p = py[2 * j + half]
                    for kk in range(8):
                        k.op("pe", lambda e: e.matmul(p[:], lhsT=Oall[:, kk, col0:col0 + 128], rhs=wo[:, kk, half * 512:(half + 1) * 512],
                                                      start=(kk == 0), stop=(kk == 7)), [BO, Bwo], [Bpy[2 * j + half]])
                    k.op("dve", lambda e: e.tensor_tensor(out=t[j][:, half * 512:(half + 1) * 512], in0=p[:],
                                                          in1=self.gbc[:, v, gi, half * 512:(half + 1) * 512], op=ALU.mult),
                         [Bpy[2 * j + half], self.Bgbc], [Bt[j]])
                k.op("dve", lambda e: e.tensor_tensor(out=t[j][:], in0=t[j][:], in1=xt[j][:], op=ALU.add), [Bt[j], Bxt[j]], [Bt[j]])
                k.dma("sp", dst, t[j][:], reads=[Bt[j]], writes=[Bdst])
                if exch is not None and bi == 0:
                    src_t, row, rowg, row2 = exch
                    Brow, Browg = Buf(), Buf()
                    k.dma("pool", row.ap(), src_t.ap()[TO - 1:TO, :], reads=[Bdst], writes=[Brow])
                    k.allgather([[0, 1], [2, 3], [4, 5], [6, 7]], row.ap(), rowg.ap(), reads=[Brow], writes=[Browg])
            if exch is not None:
                self.exchange_b(rowg, row2, Browg)
            k.barrier()

    def exchange_b(self, rowg, row2, Browg):
        k = self.k
        Brow2 = Buf()
        with ExitStack() as st:
            rg = self.sb(st, "xr_rg", [1, 2 * D])
            hr = self.sb(st, "xr_hr", [1, D])
            Brg, Bhr = Buf(), Buf()
            k.dma("sp", rg[:], rowg.ap().rearrange("a d -> (a d)").rearrange("(o n) -> o n", o=1), reads=[Browg], writes=[Brg])
            k.op("dve", lambda e: e.tensor_scalar(out=hr[:], in0=rg[:, 0:D], scalar1=self.sel[0:1, 0:1], scalar2=None, op0=ALU.mult),
                 [Brg, self.Bsel], [Bhr])
            k.op("dve", lambda e: e.scalar_tensor_tensor(out=hr[:], in0=rg[:, D:2 * D], scalar=self.sel[0:1, 1:2], in1=hr[:],
                                                         op0=ALU.mult, op1=ALU.add), [Brg, Bhr, self.Bsel], [Bhr])
            k.dma("sp", row2.ap(), hr[:], reads=[Bhr], writes=[Brow2])
            k.barrier()
        self.Brow2 = Brow2
        self.row2ap = row2.ap()

    def ffn(self, l):
        k, I, S = self.k, self.I, self.S
        cB, cF = self.cB, self.cF
        src = S["hm"] if l == 0 else S["hm1"]
        Bsrc = getattr(self, "Bd_hm" if l == 0 else "Bd_hm1")
        row2 = S["row2"] if l == 0 else S["rowg2"]
        passes = [(0, 1024, None, src.ap()[1024:1025, :], 0), (1024, 1024, src.ap()[1023:1024, :], self.row2ap, 0)]
        if l == 0:
            passes.append((TO, 256, None, None, 1))
        Bh1 = Buf("h1")
        self.Bd_h1 = Bh1
        for (r0, W, left, right, v) in passes:
            with ExitStack() as so, ExitStack() as st:
                X = self.sb(so, "f_X", [128, 8, W + 2], BF16)
                BX = Buf()
                act = self.sb(so, "f_act", [128, NJ, W], BF16)
                Bact = Buf()
                cwt = self.sb(so, "f_cw", [128, 44, 3])
                cbt = self.sb(so, "f_cb", [128, 44])
                Bcw = Buf()
                wd = self.sb(so, "f_wd", [128, NJ, D], BF16)
                Bwd = [Buf() for _ in range(NJ)]
                k.dma("sp", cwt[:], I["cw"].ap()[l], writes=[Bcw])
                k.dma("sp", cbt[:], I["cbias"].ap()[l], writes=[Bcw])
                wu = [self.sb(so, "f_wu%d" % i, [128, 8, 2, 256], BF16) for i in range(3)]
                Bwu = [Buf() for _ in range(3)]

                def wload(g):
                    g3 = g % 3
                    for part, coff in ((0, g * 256), (1, DFF + g * 256)):
                        k.dma("pool", wu[g3][:, :, part, :],
                              I["wup"].ap()[l, :, coff:coff + 256].rearrange("(k p) n -> p k n", p=128), writes=[Bwu[g3]])
                    for j_ in (2 * g, 2 * g + 1):
                        k.dma("pool", wd[:, j_, :], I["wdn"].ap()[l, j_ * 128:(j_ + 1) * 128, :], writes=[Bwd[j_]])

                wload(0)
                wload(1)
                sr = ExitStack()
                R = self.norm_res(sr, nbuf=3, npt=3)
                R["ms_eng"] = "dve"
                self.norm_seq(R, [([(src.ap()[r0 + bi * 128:r0 + (bi + 1) * 128, :], 0)], v, 1, X, BX, bi * 128, [Bsrc])
                                  for bi in range(W // 128)])
                if left is None and right is None:
                    k.op("pool", lambda e: e.memset(X[:, :, W:W + 2], 0.0), [], [BX])
                else:
                    la = left if left is not None else right
                    self.norm_block(R, [(la, 0), (right, 1)], v, 1, X, BX, W, rd=[Bsrc, self.Brow2])
                    if left is None:
                        k.op("pool", lambda e: e.memset(X[:, :, W:W + 1], 0.0), [], [BX])
                k.barrier()
                sr.close()
                hup = [self.sb(st, "f_hup%d" % i, [128, 2, W + 2]) for i in range(3)]
                Bhup = [Buf() for _ in range(3)]
                tcv = [[self.sb(st, "f_tcv%d_%d" % (i, p_), [128, W]) for p_ in range(2)] for i in range(2)]
                Btcv = [[Buf(), Buf()] for _ in range(2)]
                pu = [self.ps(st, "f_pu%d" % i) for i in range(4)]
                Bpu = [Buf() for _ in range(4)]
                ph = [self.ps(st, "f_ph%d" % i, [128, 16]) for i in range(2)]
                Bph = [Buf(), Buf()]
                pieces = [(c0, min(512, W - c0)) for c0 in range(0, W, 512)]
                cnt = [0]

                def mm(j):
                    j3 = j % 3
                    g3 = (j // 2) % 3
                    wo_ = (j % 2) * 128
                    hb = hup[j3]
                    for part in range(2):
                        for (c0, n) in pieces:
                            i4 = cnt[0] % 4
                            cnt[0] += 1
                            for kk in range(8):
                                k.op("pe", lambda e: e.matmul(pu[i4][:, :n], lhsT=wu[g3][:, kk, part, wo_:wo_ + 128],
                                                              rhs=X[:, kk, c0:c0 + n], start=(kk == 0), stop=(kk == 7)),
                                     [Bwu[g3], BX], [Bpu[i4]])
                            k.op("act", lambda e: e.activation(out=hb[:, part, 1 + c0:1 + c0 + n], in_=pu[i4][:, :n], func=AF.Copy),
                                 [Bpu[i4]], [Bhup[j3]])
                        for kk in range(8):
                            k.op("pe", lambda e: e.matmul(ph[part][:, 0:2], lhsT=wu[g3][:, kk, part, wo_:wo_ + 128],
                                                          rhs=X[:, kk, W:W + 2], start=(kk == 0), stop=(kk == 7)), [Bwu[g3], BX], [Bph[part]])
                        k.op("act", lambda e: e.activation(out=hb[:, part, 0:1], in_=ph[part][:, 0:1], func=AF.Copy),
                             [Bph[part]], [Bhup[j3]])
                        k.op("act", lambda e: e.activation(out=hb[:, part, W + 1:W + 2], in_=ph[part][:, 1:2], func=AF.Copy),
                             [Bph[part]], [Bhup[j3]])

                def conv_a(j):
                    j3, j2 = j % 3, j % 2
                    hb = hup[j3]
                    for part in range(2):
                        tb, Btb = tcv[j2][part], Btcv[j2][part]
                        ci = part * NJ + j
                        k.op("act", lambda e: e.activation(out=tb[:], in_=hb[:, part, 0:W], func=AF.Copy, scale=cwt[:, ci, 0:1]),
                             [Bhup[j3], Bcw], [Btb])
                        k.op("dve", lambda e: e.scalar_tensor_tensor(out=tb[:], in0=hb[:, part, 1:W + 1], scalar=cwt[:, ci, 1:2], in1=tb[:],
                                                                     op0=ALU.mult, op1=ALU.add), [Bhup[j3], Bcw, Btb], [Btb])
                        k.op("dve", lambda e: e.scalar_tensor_tensor(out=tb[:], in0=hb[:, part, 2:W + 2], scalar=cwt[:, ci, 2:3], in1=tb[:],
                                                                     op0=ALU.mult, op1=ALU.add), [Bhup[j3], Bcw, Btb], [Btb])

                def conv_b(j):
                    j2 = j % 2
                    tu_, tg_ = tcv[j2]
                    Btu_, Btg_ = Btcv[j2]
                    k.op("act", lambda e: e.activation(out=tg_[:], in_=tg_[:], func=AF.Silu, bias=cbt[:, NJ + j:NJ + j + 1]),
                         [Btg_, Bcw], [Btg_])
                    k.op("dve", lambda e: e.scalar_tensor_tensor(out=act[:, j, :], in0=tu_[:], scalar=cbt[:, j:j + 1], in1=tg_[:],
                                                                 op0=ALU.add, op1=ALU.mult), [Btu_, Btg_, Bcw], [Bact])

                for j in range(NJ + 2):
                    if j % 2 == 0 and j // 2 + 2 < NJ // 2:
                        wload(j // 2 + 2)
                    if j < NJ:
                        mm(j)
                    if 0 <= j - 1 < NJ:
                        conv_a(j - 1)
                    if 0 <= j - 2 < NJ:
                        conv_b(j - 2)
                k.barrier()
                st.close()
                st = so
                py = [self.ps(st, "f_py%d" % i) for i in range(4)]
                Bpy = [Buf() for _ in range(4)]
                hmb = [self.sb(st, "f_hm%d" % i, [128, D]) for i in range(2)]
                Bhmb = [Buf(), Buf()]
                tt = [self.sb(st, "f_t%d" % i, [128, D]) for i in range(2)]
                Btt = [Buf(), Buf()]
                fng = self.sb(st, "f_fng", [128, D])
                Bfng = Buf()
                fss = [self.sb(st, "f_ss%d" % i, [128, 4]) for i in range(2)]
                Bfss = [Buf(), Buf()]
                fjunk = self.sb(st, "f_junk", [128, D], BF16)
                Bfjunk = Buf()
                if l == 1:
                    k.dma("sp", fng[:], dap(I["fng"], 0, [[0, 128], [1, D]]), writes=[Bfng])
                for bi in range(W // 128):
                    jj = bi % 2
                    rr = r0 + bi * 128
                    k.dma("sp", hmb[jj][:], src.ap()[rr:rr + 128, :], reads=[Bsrc], writes=[Bhmb[jj]])
                    for half in range(2):
                        pi = 2 * jj + half
                        for j in range(NJ):
                            k.op("pe", lambda e: e.matmul(py[pi][:], lhsT=act[:, j, bi * 128:(bi + 1) * 128],
                                                          rhs=wd[:, j, half * 512:(half + 1) * 512], start=(j == 0), stop=(j == NJ - 1)),
                                 [Bact, Bwd[j]], [Bpy[pi]])
                        k.op("dve", lambda e: e.tensor_tensor(out=tt[jj][:, half * 512:(half + 1) * 512], in0=py[pi][:],
                                                              in1=self.gbc[:, v, 1, half * 512:(half + 1) * 512], op=ALU.mult),
                             [Bpy[pi], self.Bgbc], [Btt[jj]])
                    k.op("pool", lambda e: e.tensor_tensor(out=tt[jj][:], in0=tt[jj][:], in1=hmb[jj][:], op=ALU.add),
                         [Btt[jj], Bhmb[jj]], [Btt[jj]])
                    if l == 0:
                        k.dma("sp", S["h1"].ap()[rr:rr + 128, :], tt[jj][:], reads=[Btt[jj]], writes=[Bh1])
                    else:
                        ss, Bss = fss[jj], Bfss[jj]
                        k.op("dve", lambda e: e.memset(ss[:], 0.0), [], [Bss])
                        k.op("act", lambda e: e.activation(out=fjunk[:], in_=tt[jj][:], func=AF.Square, accum_out=ss[:, 0:1]),
                             [Btt[jj], Bss], [Bfjunk, Bss])
                        k.op("act", lambda e: e.activation(out=ss[:, 1:2], in_=ss[:, 0:1], func=AF.Ln, scale=1.0 / D,
                                                           bias=self.epsT[:, 0:1]), [Bss, self.Beps], [Bss])
                        k.op("act", lambda e: e.activation(out=ss[:, 2:3], in_=ss[:, 1:2], func=AF.Exp, scale=-0.5), [Bss], [Bss])
                        k.op("dve", lambda e: e.scalar_tensor_tensor(out=hmb[jj][:], in0=tt[jj][:], scalar=ss[:, 2:3], in1=fng[:],
                                                                     op0=ALU.mult, op1=ALU.mult), [Btt[jj], Bss, Bfng], [Bhmb[jj]])
                        k.dma("sp", self.y.ap()[rr:rr + 128, :], hmb[jj][:], reads=[Bhmb[jj]], writes=[self.By])
                k.barrier()

    def layer1(self):
        k, I, S = self.k, self.I, self.S
        sw = ExitStack()
        w1 = self.sb(sw, "g_w1", [128, 8, 3072], BF16)
        Bw1 = Buf()
        for g in range(6):
            k.dma("pool", w1[:, :, g * 512:(g + 1) * 512], I["w1"].ap()[:, g * 512:(g + 1) * 512].rearrange("(k p) n -> p k n", p=128),
                  writes=[Bw1])
        self.compute_mod(1, False)
        self.gla(w1, Bw1)
        sw.close()
        with ExitStack() as st:
            Oall = self.sb(st, "Oall1", [128, 8, TO], BF16)
            BO = Buf()
            k.dma("sp", Oall[:], S["oall1"].ap().rearrange("c p n -> p c n"), reads=[self.Boall1], writes=[BO])
            self.outproj(Oall, BO, I["w1o"], 0, [(S["h1"].ap()[b * 128:(b + 1) * 128, :], S["hm1"].ap()[b * 128:(b + 1) * 128, :], 0, b * 128)
                                                 for b in range(16)], "hm1", xrd=[self.Bd_h1], exch=(S["hm1"], S["row"], S["rowg"], S["row2"]))
        if self.debug == "e":
            return
        self.ffn(1)

    def gla(self, w1, Bw1):
        k, I, S = self.k, self.I, self.S
        cB, cF = self.cB, self.cF
        Bh1 = self.Bd_h1
        self.Boall1 = Buf("oall1")
        Bofwd = Buf("ofwd")
        with ExitStack() as st:
            R = self.norm_res(st)
            gw1 = self.sb(st, "g_gw1", [128, 8, 32], BF16)
            gw2 = self.sb(st, "g_gw2", [32, 2, 512], BF16)
            glng = self.sb(st, "g_glng", [128, 2])
            Bgw = Buf()
            with ExitStack() as s2:
                gw1f = self.sb(s2, "g_gw1f", [128, 8, 32])
                gw2f = self.sb(s2, "g_gw2f", [32, 2, 512])
                Bgwf = Buf()
                k.dma("sp", gw1f[:], I["gw1"].ap(), writes=[Bgwf])
                k.dma("sp", gw2f[:], I["gw2"].ap(), writes=[Bgwf])
                k.dma("sp", glng[:], I["glng"].ap(), writes=[Bgw])
                k.op("dve", lambda e: e.tensor_copy(out=gw1[:], in_=gw1f[:]), [Bgwf], [Bgw])
                k.op("dve", lambda e: e.tensor_copy(out=gw2[:], in_=gw2f[:]), [Bgwf], [Bgw])
                k.barrier()
            X = self.sb(st, "g_X", [128, 8, 512], BF16)
            BX = Buf()
            qT = self.sb(st, "g_qT", [128, 4, 512])
            kT = self.sb(st, "g_kT", [128, 4, 512])
            ktm = self.sb(st, "g_ktm", [128, 4, 512])
            vtm = self.sb(st, "g_vtm", [128, 4, 1024], BF16)
            latm = self.sb(st, "g_latm", [128, 4, 512], BF16)
            sgT = self.sb(st, "g_sgT", [128, 8, 512], BF16)
            BqT, BkT, Bktm, Bvtm, Blatm, BsgT = Buf(), Buf(), Buf(), Buf(), Buf(), Buf()
            raug = self.sb(st, "g_raug", [32, 512], BF16)
            Braug = Buf()
            k.op("pool", lambda e: e.memset(raug[:], 1.0), [], [Braug])
            ez = self.sb(st, "g_ez", [128, 512])
            Bez = Buf()
            ot = self.sb(st, "g_ot", [128, 8, 512])
            Bot = Buf()
            oo = self.sb(st, "g_oo", [128, 8, 512], BF16)
            Boo = Buf()
            sqt = self.sb(st, "g_sqt", [128, 2, 512], BF16)
            Bsqt = Buf()
            rs = self.sb(st, "g_rs", [128, 512])
            Brs = Buf()
            tf = self.sb(st, "g_tf", [128, 512])
            Btf = Buf()
            SfA = self.sb(st, "g_SfA", [128, 4, 256])
            Sf = [SfA[:, h, :] for h in range(4)]
            Sb = [self.sb(st, "g_Sb%d" % h, [128, 256], BF16) for h in range(4)]
            BSf = [Buf() for _ in range(4)]
            BSb = [Buf() for _ in range(4)]
            epm2 = [[self.sb(st, "g_epm%d_%d" % (p_, h), [128, 2, 128]) for h in range(4)] for p_ in range(2)]
            qk2 = [[self.sb(st, "g_qk%d_%d" % (p_, h), [128, 2, 128], BF16) for h in range(4)] for p_ in range(2)]
            sTt2 = [[self.sb(st, "g_sT%d_%d" % (p_, h), [128, 128], BF16) for h in range(4)] for p_ in range(2)]
            ek2 = [[self.sb(st, "g_ek%d_%d" % (p_, h), [128, 128]) for h in range(4)] for p_ in range(2)]
            kh2 = [[self.sb(st, "g_kh%d_%d" % (p_, h), [128, 128], BF16) for h in range(4)] for p_ in range(2)]
            Bepm2, Bqk2, BsTt2, Bek2, Bkh2 = ([[Buf() for _ in range(4)] for _ in range(2)] for _ in range(5))
            par = [0]
            G = [self.ps(st, "g_G%d" % i) for i in range(6)]
            BG = [Buf() for _ in range(6)]
            for h in range(4):
                k.op("pool", lambda e: e.memset(Sf[h][:], 0.0), [], [BSf[h]])
                k.op("pool", lambda e: e.memset(Sb[h][:], 0.0), [], [BSb[h]])
            gc = [0]

            def nextG():
                i = gc[0] % 6
                gc[0] += 1
                return G[i], BG[i]

            def prep(cb, d, full):
                tri = C_TRIF if d == 0 else C_TRIR
                suf = C_SUFF if d == 0 else C_PRER
                last = 127 if d == 0 else 0
                c0 = cb * 128
                pp_ = par[0] % 2
                par[0] += 1
                epm, qk, sTt, ek, kh = epm2[pp_], qk2[pp_], sTt2[pp_], ek2[pp_], kh2[pp_]
                Bepm, Bqk, BsTt, Bek, Bkh = Bepm2[pp_], Bqk2[pp_], BsTt2[pp_], Bek2[pp_], Bkh2[pp_]
                HS = [slice(h * 128, (h + 1) * 128) for h in range(4)]
                pBs = []
                for h in range(4):
                    pB, BpB = nextG()
                    pBs.append((pB, BpB))
                    k.op("pe", lambda e: e.matmul(pB[:, :128], lhsT=latm[:, cb, HS[h]], rhs=cB[:, tri:tri + 128], start=True, stop=True),
                         [Blatm, self.BcB], [BpB])
                for h in range(4):
                    pB, BpB = pBs[h]
                    k.op("act", lambda e: e.activation(out=epm[h][:, 0, :], in_=pB[:, :128], func=AF.Exp), [BpB], [Bepm[h]])
                    if full:
                        k.op("act", lambda e: e.activation(out=epm[h][:, 1, :], in_=pB[:, :128], func=AF.Exp, scale=-1.0), [BpB], [Bepm[h]])
                pXs = []
                for h in range(4):
                    pX, BpX = nextG()
                    pXs.append((pX, BpX))
                    k.op("pe", lambda e: e.matmul(pX[:, :128], lhsT=cB[:, suf:suf + 128], rhs=latm[:, cb, HS[h]], start=True, stop=True),
                         [Blatm, self.BcB], [BpX])
                for h in range(4):
                    pX, BpX = pXs[h]
                    k.op("act", lambda e: e.activation(out=ek[h][:], in_=pX[:, :128], func=AF.Exp), [BpX], [Bek[h]])
                if full:
                    for h in range(4):
                        k.op("dve", lambda e: e.tensor_tensor(out=qk[h][:, 0, :], in0=qT[:, h, c0:c0 + 128], in1=epm[h][:, 0, :], op=ALU.mult),
                             [BqT, Bepm[h]], [Bqk[h]])
                        k.op("dve", lambda e: e.tensor_tensor(out=qk[h][:, 1, :], in0=kT[:, h, c0:c0 + 128], in1=epm[h][:, 1, :], op=ALU.mult),
                             [BkT, Bepm[h]], [Bqk[h]])
                    pSs = []
                    for h in range(4):
                        pS, BpS = nextG()
                        pSs.append((pS, BpS))
                        k.op("pe", lambda e: e.matmul(pS[:, :128], lhsT=qk[h][:, 1, :], rhs=qk[h][:, 0, :], start=True, stop=True),
                             [Bqk[h]], [BpS])
                for h in range(4):
                    k.op("dve", lambda e: e.tensor_tensor(out=kh[h][:], in0=ktm[:, cb, HS[h]], in1=ek[h][:], op=ALU.mult),
                         [Bktm, Bek[h]], [Bkh[h]])
                if full:
                    for h in range(4):
                        pS, BpS = pSs[h]
                        k.op("dve", lambda e: e.tensor_tensor(out=sTt[h][:], in0=pS[:, :128], in1=cB[:, tri:tri + 128], op=ALU.mult),
                             [BpS, self.BcB], [BsTt[h]])
                return dict(cb=cb, d=d, full=full, pp_=pp_, last=last, c0=c0)

            def serial(cx):
                cb, d, full, pp_, last, c0 = cx["cb"], cx["d"], cx["full"], cx["pp_"], cx["last"], cx["c0"]
                epm, qk, sTt, ek, kh = epm2[pp_], qk2[pp_], sTt2[pp_], ek2[pp_], kh2[pp_]
                Bepm, Bqk, BsTt, Bek, Bkh = Bepm2[pp_], Bqk2[pp_], BsTt2[pp_], Bek2[pp_], Bkh2[pp_]
                if full:
                    for h in range(4):
                        for e2 in range(2):
                            pO, BpO = nextG()
                            vs_ = slice(h * 256 + e2 * 128, h * 256 + (e2 + 1) * 128)
                            k.op("pe", lambda e: e.matmul(pO[:, :128], lhsT=vtm[:, cb, vs_], rhs=sTt[h][:], start=True, stop=False),
                                 [Bvtm, BsTt[h]], [BpO])
                            k.op("pe", lambda e: e.matmul(pO[:, :128], lhsT=Sb[h][:, e2 * 128:(e2 + 1) * 128], rhs=qk[h][:, 0, :],
                                                          start=False, stop=True), [BSb[h], Bqk[h]], [BpO])
                            if d == 0:
                                k.op("act", lambda e: e.activation(out=ot[:, 2 * h + e2, c0:c0 + 128], in_=pO[:, :128], func=AF.Copy),
                                     [BpO], [Bot])
                            else:
                                k.op("dve", lambda e: e.tensor_tensor(out=ot[:, 2 * h + e2, c0:c0 + 128], in0=ot[:, 2 * h + e2, c0:c0 + 128],
                                                                      in1=pO[:, :128], op=ALU.add), [BpO, Bot], [Bot])
                for h in range(4):
                    pD, BpD = nextG()
                    k.op("pe", lambda e: e.matmul(pD[:, :256], lhsT=kh[h][:], rhs=vtm[:, cb, h * 256:(h + 1) * 256], start=True, stop=True),
                         [Bkh[h], Bvtm], [BpD])
                    k.op("dve", lambda e: e.scalar_tensor_tensor(out=Sf[h][:], in0=Sf[h][:], scalar=epm[h][:, 0, last:last + 1], in1=pD[:, :256],
                                                                 op0=ALU.mult, op1=ALU.add), [BSf[h], Bepm[h], BpD], [BSf[h]])
                    k.op("act", lambda e: e.activation(out=Sb[h][:], in_=Sf[h][:], func=AF.Copy), [BSf[h]], [BSb[h]])

            def xblock(r0, v, b):
                self.norm_block(R, [(S["h1"].ap()[r0 + b * 128:r0 + (b + 1) * 128, :], 0)], v, 0, X, BX, b * 128, rd=[Bh1])

            def xblock_a(r0, v, b):
                return self.norm_a(R, [(S["h1"].ap()[r0 + b * 128:r0 + (b + 1) * 128, :], 0)], v, 0, X, BX, b * 128, rd=[Bh1])

            def tile(r0, v, d, full, nblk, build_x=True, nxt=None):
                n = nblk * 128
                if build_x:
                    for b in range(nblk):
                        xblock(r0, v, b)
                if full:
                    for h in range(4):
                        for (dst, Bdst, coff, sc) in ((qT, BqT, h * 128, 128 ** -0.5), (kT, BkT, 512 + h * 128, 1.0)):
                            p, Bp = nextG()
                            for kk in range(8):
                                k.op("pe", lambda e: e.matmul(p[:, :n], lhsT=w1[:, kk, coff:coff + 128], rhs=X[:, kk, :n],
                                                              start=(kk == 0), stop=(kk == 7)), [Bw1, BX], [Bp])
                            k.op("act", lambda e: e.activation(out=dst[:, h, :n], in_=p[:, :n], func=AF.Copy, scale=sc), [Bp], [Bdst])
                    if d == 1:
                        for c in range(8):
                            p, Bp = nextG()
                            for kk in range(8):
                                k.op("pe", lambda e: e.matmul(p[:, :n], lhsT=w1[:, kk, 2048 + c * 128:2048 + (c + 1) * 128], rhs=X[:, kk, :n],
                                                              start=(kk == 0), stop=(kk == 7)), [Bw1, BX], [Bp])
                            k.op("act", lambda e: e.activation(out=sgT[:, c, :n], in_=p[:, :n], func=AF.Silu), [Bp], [BsgT])
                for b in range(nblk):
                    bs = slice(b * 128, (b + 1) * 128)
                    p, Bp = nextG()
                    for kk in range(8):
                        k.op("pe", lambda e: e.matmul(p[:, :], lhsT=X[:, kk, bs], rhs=w1[:, kk, 512:1024], start=(kk == 0), stop=(kk == 7)),
                             [Bw1, BX], [Bp])
                    k.op("act", lambda e: e.activation(out=ktm[:, b, :], in_=p[:, :], func=AF.Copy), [Bp], [Bktm])
                    for i2 in range(2):
                        p, Bp = nextG()
                        for kk in range(8):
                            k.op("pe", lambda e: e.matmul(p[:, :], lhsT=X[:, kk, bs], rhs=w1[:, kk, 1024 + i2 * 512:1536 + i2 * 512],
                                                          start=(kk == 0), stop=(kk == 7)), [Bw1, BX], [Bp])
                        k.op("act", lambda e: e.activation(out=vtm[:, b, i2 * 512:(i2 + 1) * 512], in_=p[:, :], func=AF.Copy), [Bp], [Bvtm])
                p, Bp = nextG()
                for kk in range(8):
                    k.op("pe", lambda e: e.matmul(p[0:16, :n], lhsT=gw1[:, kk, d * 16:(d + 1) * 16], rhs=X[:, kk, :n],
                                                  start=(kk == 0), stop=(kk == 7)), [Bgw, BX], [Bp])
                k.op("act", lambda e: e.activation(out=raug[0:16, :n], in_=p[0:16, :n], func=AF.Copy), [Bp], [Braug])
                for b in range(nblk):
                    p, Bp = nextG()
                    k.op("pe", lambda e: e.matmul(p[:, :], lhsT=raug[:, b * 128:(b + 1) * 128], rhs=gw2[:, d, :], start=True, stop=True),
                         [Braug, Bgw], [Bp])
                    k.op("act", lambda e: e.activation(out=ez[:], in_=p[:, :], func=AF.Exp, scale=-1.0), [Bp], [Bez])
                    k.op("act", lambda e: e.activation(out=ez[:], in_=ez[:], func=AF.Ln, bias=cF[:, C_ONE:C_ONE + 1]), [Bez, self.BcF], [Bez])
                    k.op("dve", lambda e: e.tensor_scalar(out=latm[:, b, :], in0=ez[:], scalar1=-1.0 / 16.0, scalar2=None, op0=ALU.mult),
                         [Bez], [Blatm])
                order = list(range(nblk) if d == 0 else range(nblk - 1, -1, -1))
                nxt_blocks = [] if nxt is None else [(nxt[0], nxt[1], b) for b in range(nxt[2])]
                cx = prep(order[0], d, full)
                pend = None
                for i in range(len(order)):
                    cxn = prep(order[i + 1], d, full) if i + 1 < len(order) else None
                    if pend is not None:
                        self.norm_b(R, pend)
                        pend = None
                    if nxt_blocks:
                        pend = xblock_a(*nxt_blocks.pop(0))
                    serial(cx)
                    cx = cxn
                while nxt_blocks or pend is not None:
                    if pend is not None:
                        self.norm_b(R, pend)
                        pend = None
                    if nxt_blocks:
                        pend = xblock_a(*nxt_blocks.pop(0))

            tile(TO, 1, 0, False, 2, build_x=True, nxt=(0, 0, 4))
            for ti in range(4):
                nx = ((ti + 1) * 512, 0, 4) if ti < 3 else (3 * 512, 0, 4)
                tile(ti * 512, 0, 0, True, 4, build_x=False, nxt=nx)
                k.dma("sp", S["ofwd"].ap()[:, :, ti * 512:(ti + 1) * 512].rearrange("c p n -> p c n"), ot[:], reads=[Bot], writes=[Bofwd])
            Bst, Bstg = Buf(), Buf()
            k.dma("sp", S["st"].ap(), SfA[:].rearrange("p h n -> p (h n)"), reads=[BSf[0], BSf[1], BSf[2], BSf[3]], writes=[Bst])
            k.allgather([[0, 1], [2, 3], [4, 5], [6, 7]], S["st"].ap(), S["stg"].ap(), reads=[Bst], writes=[Bstg])
            sg2 = ot[:].rearrange("p c n -> p (c n)")[:, 0:2048].rearrange("p (r n) -> p r n", r=2)
            k.dma("sp", sg2, S["stg"].ap().rearrange("(r p) n -> p r n", p=128), reads=[Bstg, Bofwd], writes=[Bot])
            for h in range(4):
                k.op("dve", lambda e: e.tensor_scalar(out=Sf[h][:], in0=sg2[:, 0, h * 256:(h + 1) * 256], scalar1=self.sel[:, 0:1],
                                                      scalar2=None, op0=ALU.mult), [Bot, self.Bsel], [BSf[h]])
                k.op("dve", lambda e: e.scalar_tensor_tensor(out=Sf[h][:], in0=sg2[:, 1, h * 256:(h + 1) * 256], scalar=self.sel[:, 1:2],
                                                             in1=Sf[h][:], op0=ALU.mult, op1=ALU.add), [Bot, self.Bsel, BSf[h]], [BSf[h]])
                k.op("act", lambda e: e.activation(out=Sb[h][:], in_=Sf[h][:], func=AF.Copy), [BSf[h]], [BSb[h]])
            for ti in range(3, -1, -1):
                k.dma("sp", ot[:], S["ofwd"].ap()[:, :, ti * 512:(ti + 1) * 512].rearrange("c p n -> p c n"), reads=[Bofwd, BSf[0], BSf[1], BSf[2], BSf[3]],
                      writes=[Bot])
                tile(ti * 512, 0, 1, True, 4, build_x=False, nxt=(((ti - 1) * 512, 0, 4) if ti > 0 else None))
                for h in range(4):
                    k.op("act", lambda e: e.activation(out=sqt[:], in_=ot[:, 2 * h:2 * h + 2, :], func=AF.Square), [Bot], [Bsqt])
                    pN, BpN = nextG()
                    for e2 in range(2):
                        k.op("pe", lambda e: e.matmul(pN[:, :], lhsT=cB[:, C_ONE:C_ONE + 128], rhs=sqt[:, e2, :], start=(e2 == 0), stop=(e2 == 1)),
                             [self.BcB, Bsqt], [BpN])
                    k.op("act", lambda e: e.activation(out=rs[:], in_=pN[:, :], func=AF.Ln, scale=1.0 / 256, bias=self.epsT[:, 0:1]),
                         [BpN, self.Beps], [Brs])
                    k.op("act", lambda e: e.activation(out=rs[:], in_=rs[:], func=AF.Exp, scale=-0.5), [Brs], [Brs])
                    for e2 in range(2):
                        c = 2 * h + e2
                        k.op("dve", lambda e: e.scalar_tensor_tensor(out=tf[:], in0=ot[:, c, :], scalar=glng[:, e2:e2 + 1], in1=rs[:],
                                                                     op0=ALU.mult, op1=ALU.mult), [Bot, Bgw, Brs], [Btf])
                        k.op("dve", lambda e: e.tensor_tensor(out=oo[:, c, :], in0=tf[:], in1=sgT[:, c, :], op=ALU.mult), [Btf, BsgT], [Boo])
                k.dma("sp", S["oall1"].ap()[:, :, ti * 512:(ti + 1) * 512].rearrange("c p n -> p c n"), oo[:], reads=[Boo], writes=[self.Boall1])
            k.barrier()


def _consts():
    c = np.zeros((128, NCOL), np.float32)
    r = np.arange(128)
    c[r, C_ID + r] = 1.0
    sw = np.where(r % 64 < 32, r + 32, r - 32)
    c[sw, C_PERM + r] = 1.0
    c[:, C_BD:C_BD + 128] = (r[:, None] // 64 == r[None, :] // 64)
    c[0, C_ELO:C_ELO + 128] = 1.0
    c[64, C_EHI:C_EHI + 128] = 1.0
    c[:, C_OLO:C_OLO + 64] = 1.0
    c[:, C_OHI + 64:C_OHI + 128] = 1.0
    c[:, C_ONE:C_ONE + 128] = 1.0
    u = np.arange(640)
    c[:, C_BAND:C_BAND + 640] = ((u[None, :] >= r[:, None] + 128) & (u[None, :] <= r[:, None] + 384))
    s_, t_ = r[:, None], r[None, :]
    c[:, C_TRIF:C_TRIF + 128] = (s_ <= t_)
    c[:, C_SUFF:C_SUFF + 128] = (s_ > t_)
    c[:, C_TRIR:C_TRIR + 128] = (s_ >= t_)
    c[:, C_PRER:C_PRER + 128] = (s_ < t_)
    return c


def _rope_tables():
    rows = T // 64
    row = np.repeat(np.arange(rows, dtype=np.float32), 64)
    col = np.tile(np.arange(64, dtype=np.float32), rows)
    inv = (10000.0 ** (-np.arange(0, 32, 2, dtype=np.float32) / 32)).astype(np.float32)
    ang = np.concatenate([row[:, None] * inv, col[:, None] * inv], axis=-1).astype(np.float32)
    return np.cos(ang).astype(np.float32), np.sin(ang).astype(np.float32)


def _perm64():
    return np.concatenate([np.arange(0, 64, 2), np.arange(1, 64, 2)])


def _prep(inp):
    f = lambda a: np.ascontiguousarray(np.asarray(a, dtype=np.float32))
    x, c, ctx, c_ctx = f(inp["x"]), f(inp["c"]), f(inp["ctx"]), f(inp["c_ctx"])
    mod_w, mod_b = f(inp["mod_w"]), f(inp["mod_b"])
    p64 = _perm64()
    w_in = f(inp["attn_w_in"])[0]
    aq, ak, av, bq, bk, bv = np.split(w_in, [512, 640, 768, 1280, 1792], axis=1)
    cols = []
    for cchunk in range(4):
        for j in range(2):
            h = 2 * cchunk + j
            cols.append(aq[:, h * 64:(h + 1) * 64][:, p64])
    for kv in range(2):
        for j in range(2):
            cols.append(ak[:, kv * 64:(kv + 1) * 64][:, p64])
    for h in range(4):
        for m in range(2):
            cols.append(bq[:, (2 * h + m) * 64:(2 * h + m + 1) * 64][:, p64])
    for h in range(4):
        for m in range(2):
            cols.append(bk[:, (2 * h + m) * 64:(2 * h + m + 1) * 64][:, p64])
    cols.append(bv)
    cols.append(av)
    w0 = np.ascontiguousarray(np.concatenate(cols, axis=1))
    assert w0.shape == (D, 2432)
    cosT, sinT = _rope_tables()
    consts = _consts()
    fm = lambda vec, nk: np.ascontiguousarray(vec.reshape(nk, 128).T)
    shared = {
        "mod_w": mod_w, "mod_b": mod_b,
        "mod_bfm": np.stack([fm(mod_b[l], 48) for l in range(2)]),
        "n1g": np.stack([fm(f(inp["norm1_g"])[l], 8) for l in range(2)]),
        "n2g": np.stack([fm(f(inp["norm2_g"])[l], 8) for l in range(2)]),
        "fng": f(inp["final_norm_g"]),
        "w0": w0, "w0o": f(inp["attn_w_out"])[0],
        "sink": f(inp["attn_sink"])[0], "dlam": f(inp["diff_lambda"])[0].reshape(256),
        "subg": f(inp["diff_subln_g"])[0].reshape(128, 1),
        "consts": consts,
        "w1": f(inp["gla_w_in"])[0],
        "glng": fm(f(inp["gla_norm_g"])[0], 2),
        "w1o": f(inp["gla_w_out"])[0],
        "wup": f(inp["ffn_w_up"]), "wdn": f(inp["ffn_w_down"]),
        "cbias": np.stack([fm(f(inp["ffn_conv_b"])[l], 44) for l in range(2)]),
    }
    gw1 = f(inp["gla_gate_w1"])[0]
    gw2 = f(inp["gla_gate_w2"])[0]
    gb = f(inp["gla_gate_b"])[0]
    cwt = f(inp["ffn_conv_w"])
    maps = []
    for core in range(8):
        b, s = core // 2, core % 2
        xl = x[b]
        cl = ctx[b]
        tok = np.arange(T)
        dirs = [0, 1]
        cw = cwt
        if s == 1:
            xl = xl[::-1]
            cl = cl[::-1]
            tok = tok[::-1]
            dirs = [1, 0]
            cw = cwt[:, ::-1, :]
        m = dict(shared)
        m["xin"] = np.ascontiguousarray(np.concatenate([xl, cl], axis=0))
        cv = np.stack([fm(c[b], 8), fm(c_ctx, 8)], axis=-1)
        m["cvec"] = np.ascontiguousarray(cv)
        ct = np.ones((128, NTOK), np.float32)
        stb = np.zeros((128, NTOK), np.float32)
        rr = np.arange(128)
        ct[:, :T] = cosT[tok][:, rr % 32].T
        sgn = np.where(rr % 64 < 32, -1.0, 1.0).astype(np.float32)
        stb[:, :T] = sinT[tok][:, rr % 32].T * sgn[:, None]
        m["ctab"], m["stab"] = ct, stb
        sel = np.zeros((128, 2), np.float32)
        sel[:, 1 - s] = 1.0
        m["sel"] = sel
        g1 = np.zeros((128, 8, 32), np.float32)
        g2 = np.zeros((32, 2, 512), np.float32)
        for ld, d in enumerate(dirs):
            g1[:, :, ld * 16:(ld + 1) * 16] = gw1[d].reshape(8, 128, 16).transpose(1, 0, 2)
            g2[0:16, ld, :] = gw2[d]
            g2[16, ld, :] = gb[d]
        m["gw1"], m["gw2"] = g1, g2
        m["cw"] = np.ascontiguousarray(
            np.stack([cw[l].T.reshape(44, 128, 3).transpose(1, 0, 2) for l in range(2)]))
        maps.append(m)
    return maps


_CACHE = {}


def _run(inputs, debug=None):
    key = debug
    if key not in _CACHE:
        p = Prog(debug)
        nc = p.build()
        _CACHE[key] = (p, nc)
    p, nc = _CACHE[key]
    maps = _prep(inputs)
    res = run_bass_kernel_spmd(nc, maps, core_ids=list(range(8)))
    return p, res


def kernel(**inputs):
    p, res = _run(inputs, None)
    out = np.zeros((4, T, D), np.float32)
    for core in range(8):
        b, s = core // 2, core % 2
        y = np.asarray(res.results[core]["y"], dtype=np.float32)
        if s == 0:
            out[b, :TO] = y
        else:
            out[b, TO:] = y[::-1]
    return out
```
